# Optimizing a Trainium2 kernel written in Bass

```python
import math
import jax, jax.numpy as jnp
from jax import lax
import numpy as np

D_MODEL = 1024
BATCH = 1
SEQ = 16384
DEPTH = 4

HEAD_DIM = 64
N_HEADS_SB = 4
N_HEADS_DIL = 12
W_SB = N_HEADS_SB * HEAD_DIM
W_DIL = N_HEADS_DIL * HEAD_DIM
MIX_WIDTH = W_SB + W_DIL
IN_COLS = 4 * W_SB + 4 * W_DIL
DIL_PATTERNS = ((128, 1), (512, 4), (2048, 16))
ROPE_THETA = 500000.0
ROPE_DIM = HEAD_DIM // 4
BLOCK = 128
EPS = 1e-6

kernel_name = "hybrid_stickbreak_dilated_gated"


def _rmsnorm(x, g):
    xf = x.astype(jnp.float32)
    y = xf * lax.rsqrt(jnp.mean(xf * xf, axis=-1, keepdims=True) + EPS)
    return (y * g.astype(jnp.float32)).astype(x.dtype)


def _partial_rope(x, pos):
    half = ROPE_DIM // 2
    inv_freq = 1.0 / (ROPE_THETA ** (jnp.arange(half, dtype=jnp.float32) * 2.0 / ROPE_DIM))
    ang = pos.astype(jnp.float32)[:, None] * inv_freq[None, :]
    cos = jnp.cos(ang)[None, :, None, :]
    sin = jnp.sin(ang)[None, :, None, :]
    xf = x.astype(jnp.float32)
    x1 = xf[..., :half]
    x2 = xf[..., half:ROPE_DIM]
    rot = jnp.concatenate([x1 * cos - x2 * sin, x2 * cos + x1 * sin], axis=-1)
    return jnp.concatenate([rot, xf[..., ROPE_DIM:]], axis=-1).astype(x.dtype)


def _stick_breaking(q, k, v):
    B, S, H, Dh = q.shape
    nblk = S // BLOCK
    scale = 1.0 / math.sqrt(Dh)
    qt = q.transpose(0, 2, 1, 3)
    kt = k.transpose(0, 2, 1, 3)
    vt = v.transpose(0, 2, 1, 3)
    idx = jnp.arange(BLOCK)
    tri_in = (idx[:, None] > idx[None, :]).astype(jnp.float32)
    blk_idx = jnp.arange(nblk)
    tri_blk = (blk_idx[:, None] > blk_idx[None, :]).astype(jnp.float32)
    outs = []
    for i in range(nblk):
        nk = i + 1
        lk = nk * BLOCK
        qblk = qt[:, :, i * BLOCK:(i + 1) * BLOCK]
        z = jnp.einsum('bhqd,bhkd->bhqk', qblk, kt[:, :, :lk],
                       preferred_element_type=jnp.float32) * scale
        qpos = i * BLOCK + idx
        mask = jnp.arange(lk)[None, :] < qpos[:, None]
        ls_z = jax.nn.log_sigmoid(z)
        log_surv = jnp.where(mask, ls_z - z, 0.0)
        lsb = log_surv.reshape(B, H, BLOCK, nk, BLOCK)
        within = jnp.einsum('bhqnj,js->bhqns', lsb, tri_in,
                            precision=lax.Precision.HIGHEST)
        totals = jnp.sum(lsb, axis=-1)
        later = jnp.einsum('bhqm,mn->bhqn', totals, tri_blk[:nk, :nk],
                           precision=lax.Precision.HIGHEST)
        acc = (within + later[..., None]).reshape(B, H, BLOCK, lk)
        a = jnp.where(mask, jnp.exp(ls_z + acc), 0.0)
        o = jnp.einsum('bhqk,bhkd->bhqd', a, vt[:, :, :lk],
                       preferred_element_type=jnp.float32)
        outs.append(o.astype(q.dtype))
    o = jnp.concatenate(outs, axis=2)
    return o.transpose(0, 2, 1, 3)


def _dilated_window(q, k, v, window, dilation):
    B, S, H, Dh = q.shape
    r = dilation
    span = window // r
    L = S // r
    nb = -(-L // BLOCK)
    Lp = nb * BLOCK
    scale = 1.0 / math.sqrt(Dh)

    def to_blocks(t):
        t = t.reshape(B, L, r, H, Dh).transpose(0, 2, 1, 3, 4)
        t = jnp.pad(t, ((0, 0), (0, 0), (0, Lp - L), (0, 0), (0, 0)))
        return t.reshape(B, r, nb, BLOCK, H, Dh)

    def with_prev(t):
        prev = jnp.concatenate([jnp.zeros_like(t[:, :, :1]), t[:, :, :-1]], axis=2)
        return jnp.concatenate([prev, t], axis=3)

    qb = to_blocks(q)
    kk = with_prev(to_blocks(k))
    vv = with_prev(to_blocks(v))

    i = jnp.arange(BLOCK)[:, None]
    j = jnp.arange(2 * BLOCK)[None, :]
    n = jnp.arange(nb)[:, None, None]
    dist = BLOCK + i - j
    mask = (dist >= 0) & (dist <= span) & ((n > 0) | (j >= BLOCK))
    mask = mask[None, None, :, None]

    s = jnp.einsum('brnqhd,brnkhd->brnhqk', qb, kk,
                   preferred_element_type=jnp.float32) * scale
    s = jnp.where(mask, s, -jnp.inf)
    m = jnp.max(s, axis=-1, keepdims=True)
    p = jnp.exp(s - m)
    l = jnp.sum(p, axis=-1, keepdims=True)
    o = jnp.einsum('brnhqk,brnkhd->brnqhd', p, vv,
                   preferred_element_type=jnp.float32)
    o = o / jnp.transpose(l, (0, 1, 2, 4, 3, 5))
    log_den = (m + jnp.log(l))[..., 0].transpose(0, 1, 2, 4, 3)

    o = o.reshape(B, r, Lp, H, Dh)[:, :, :L].transpose(0, 2, 1, 3, 4).reshape(B, S, H, Dh)
    log_den = log_den.reshape(B, r, Lp, H)[:, :, :L].transpose(0, 2, 1, 3).reshape(B, S, H)
    return o, log_den


def _dilated_mixture(q, k, v):
    outs, dens = [], []
    for window, dilation in DIL_PATTERNS:
        o, ld = _dilated_window(q, k, v, window, dilation)
        outs.append(o)
        dens.append(ld)
    outs = jnp.stack(outs, axis=0)
    alpha = jax.nn.softmax(jnp.stack(dens, axis=0), axis=0)
    return jnp.sum(alpha[..., None] * outs, axis=0).astype(q.dtype)


def setup_inputs(seed: int = 0) -> dict:
    key = jax.random.key(seed)
    ks = jax.random.split(key, 6)
    x = jax.random.normal(ks[0], (BATCH, SEQ, D_MODEL), jnp.float32)
    norm_g = 1.0 + 0.02 * jax.random.normal(ks[1], (DEPTH, D_MODEL), jnp.float32)
    w_in = jax.random.normal(ks[2], (DEPTH, D_MODEL, IN_COLS), jnp.float32) * D_MODEL ** -0.5
    q_norm_g = 1.0 + 0.02 * jax.random.normal(ks[3], (DEPTH, HEAD_DIM), jnp.float32)
    k_norm_g = 1.0 + 0.02 * jax.random.normal(ks[4], (DEPTH, HEAD_DIM), jnp.float32)
    w_out = jax.random.normal(ks[5], (DEPTH, MIX_WIDTH, D_MODEL), jnp.float32) * (
        MIX_WIDTH ** -0.5 * (2 * DEPTH) ** -0.5)
    return {"x": x, "norm_g": norm_g, "w_in": w_in, "q_norm_g": q_norm_g,
            "k_norm_g": k_norm_g, "w_out": w_out}


def reference(x, norm_g, w_in, q_norm_g, k_norm_g, w_out):
    B, S, _ = x.shape
    pos = jnp.arange(S)
    cuts = np.cumsum([W_SB] * 4 + [W_DIL] * 3).tolist()
    for layer in range(DEPTH):
        h = _rmsnorm(x, norm_g[layer])
        proj = jnp.einsum('bsd,dc->bsc', h, w_in[layer])
        qa, ka, va, ga, qd, kd, vd, gd = jnp.split(proj, cuts, axis=-1)
        heads = lambda t, n: t.reshape(B, S, n, HEAD_DIM)

        oa = _stick_breaking(heads(qa, N_HEADS_SB), heads(ka, N_HEADS_SB), heads(va, N_HEADS_SB))
        oa = oa.reshape(B, S, W_SB) * jax.nn.silu(ga)

        qd = _partial_rope(_rmsnorm(heads(qd, N_HEADS_DIL), q_norm_g[layer]), pos)
        kd = _partial_rope(_rmsnorm(heads(kd, N_HEADS_DIL), k_norm_g[layer]), pos)
        od = _dilated_mixture(qd, kd, heads(vd, N_HEADS_DIL))
        od = od.reshape(B, S, W_DIL) * jax.nn.silu(gd)

        y = jnp.einsum('bsc,cd->bsd', jnp.concatenate([oa, od], axis=-1), w_out[layer])
        x = x + y.astype(x.dtype)
    return x
```

```python
import contextlib
import numpy as np
import ml_dtypes
import concourse.bass as bass
import concourse.mybir as mybir
from concourse.bass_utils import run_bass_kernel_spmd

F32 = mybir.dt.float32
BF16 = mybir.dt.bfloat16
AF = mybir.ActivationFunctionType
ALU = mybir.AluOpType
AX = mybir.AxisListType

NCORES = 8
SEQ = 16384
DM = 1024
DEPTH = 4
CH = SEQ // NCORES
NB = CH // 128
NBLK = SEQ // 128
EPS = 1e-6
NEG = -30000.0
FORCE_MASK = True

ENGS = ("pe", "act", "dve", "pool", "sp")


class Op:
    __slots__ = ("eng", "fn", "deps", "sig", "dma_sem", "has_dependents")

    def __init__(self, eng, fn, deps, dma_sem=None):
        self.eng = eng
        self.fn = fn
        self.deps = [d for d in deps if d is not None]
        self.dma_sem = dma_sem
        self.sig = None
        self.has_dependents = False


class Sched:
    def __init__(self, nc, name="s"):
        self.nc = nc
        self.name = name
        self.ops = {e: [] for e in ENGS}

    def op(self, eng, fn, deps=()):
        o = Op(eng, fn, deps)
        for d in o.deps:
            d.has_dependents = True
        self.ops[eng].append(o)
        return o

    def dma(self, eng, fn, sem_key, deps=()):
        o = Op(eng, fn, deps, dma_sem=sem_key)
        for d in o.deps:
            d.has_dependents = True
        self.ops[eng].append(o)
        return o

    def emit(self, final_waits=()):
        nc = self.nc
        for o in final_waits:
            o.has_dependents = True
        with contextlib.ExitStack() as st:
            esem = {e: st.enter_context(nc.semaphore(f"{self.name}_{e}")) for e in ENGS}
            keys = []
            for e in ENGS:
                for o in self.ops[e]:
                    if o.dma_sem is not None and o.dma_sem not in keys:
                        keys.append(o.dma_sem)
            dsem = {k: st.enter_context(nc.semaphore(f"{self.name}_d{i}")) for i, k in enumerate(keys)}
            dcount = {k: 0 for k in keys}
            for e in ENGS:
                c = 0
                for o in self.ops[e]:
                    if o.dma_sem is not None:
                        dcount[o.dma_sem] += 16
                        o.sig = (dsem[o.dma_sem], dcount[o.dma_sem])
                    elif o.has_dependents:
                        c += 1
                        o.sig = (esem[e], c)
            block = st.enter_context(nc.Block())

            def run(e, eng):
                waited = {}
                for o in self.ops[e]:
                    for d in o.deps:
                        sem, val = d.sig
                        if waited.get(id(sem), 0) < val:
                            eng.wait_ge(sem, val)
                            waited[id(sem)] = val
                    ins = o.fn(eng)
                    if o.sig is not None:
                        ins.then_inc(o.sig[0], 16 if o.dma_sem is not None else 1)
                if e == "sp":
                    for o in final_waits:
                        sem, val = o.sig
                        if waited.get(id(sem), 0) < val:
                            eng.wait_ge(sem, val)
                            waited[id(sem)] = val

            @block.tensor
            def _(eng):
                run("pe", eng)

            @block.scalar
            def _(eng):
                run("act", eng)

            @block.vector
            def _(eng):
                run("dve", eng)

            @block.gpsimd
            def _(eng):
                run("pool", eng)

            @block.sync
            def _(eng):
                run("sp", eng)


class Ring:
    def __init__(self, bufs):
        self.bufs = bufs
        self.i = 0
        self.readers = [[] for _ in bufs]

    def next(self):
        j = self.i % len(self.bufs)
        self.i += 1
        deps = self.readers[j]
        self.readers[j] = []
        return j, self.bufs[j], deps

    def release(self, j, ops):
        self.readers[j].extend(ops)


def emit_P(nc, st0, x_d, wg_d, win_d, gqk_d, cs_d, ident_d,
           qTs_d, kTs_d, vs_d, qTd_d, kTd_d, vd_d, gs_d, nblocks=NB):
    with contextlib.ExitStack() as st:
        T = lambda n, s, d: st.enter_context(nc.sbuf_tensor("P_" + n, s, d))
        PS = lambda n, s, d: st.enter_context(nc.psum_tensor("P_" + n, s, d))
        W = T("W", [128, 8, 4096], BF16)
        wst = [T(f"wst{i}", [128, 2048], F32) for i in range(2)]
        wg = T("wg", [128, 8], F32)
        gqk = T("gqk", [128, 1536], F32)
        cs = T("cs", [128, NB, 16], F32)
        ident = T("ident", [128, 128], BF16)
        xs = [T(f"xs{i}", [128, 1024], F32) for i in range(2)]
        junk = T("junk", [128, 1024], BF16)
        xb = T("xb", [128, 1024], BF16)
        xT = [T(f"xT{i}", [128, 8, 128], BF16) for i in range(2)]
        stat = [T(f"stat{i}", [128, 4], F32) for i in range(2)]
        qks = T("qks", [128, 512], BF16)
        vsb = [T(f"vsb{i}", [128, 256], BF16) for i in range(2)]
        gsb = [T(f"gsb{i}", [128, 1024], F32) for i in range(2)]
        qkd = T("qkd", [128, 1536], F32)
        sq = T("sq", [128, 1536], F32)
        ssh = T("ssh", [128, 24], F32)
        rsh = T("rsh", [128, 24], F32)
        rt = [T(f"rt{i}", [128, 24, 8], F32) for i in range(4)]
        qkb = T("qkb", [128, 1536], BF16)
        vdb = [T(f"vdb{i}", [128, 12, 80], BF16) for i in range(2)]
        qTs = [T(f"qTs{i}", [128, 2, 128], BF16) for i in range(2)]
        kTs = [T(f"kTs{i}", [128, 2, 128], BF16) for i in range(2)]
        qTd = [T(f"qTd{i}", [128, 6, 128], BF16) for i in range(2)]
        kTd = [T(f"kTd{i}", [128, 6, 128], BF16) for i in range(2)]
        pxT = PS("pxT", [128, 1024], BF16)
        pqT = [PS(f"pqT{i}", [128, 1024], BF16) for i in range(2)]
        ppj = [PS(f"ppj{i}", [128, 512], F32) for i in range(3)]

        s = Sched(nc, "P")
        l_wg = s.dma("sp", lambda g: g.dma_start(out=wg[:], in_=wg_d), "c0")
        l_gqk = s.dma("sp", lambda g: g.dma_start(out=gqk[:], in_=gqk_d), "c0")
        l_cs = s.dma("sp", lambda g: g.dma_start(out=cs[:], in_=cs_d), "c0")
        l_id = s.dma("sp", lambda g: g.dma_start(out=ident[:], in_=ident_d), "c0")
        l_wg = l_gqk = l_cs = l_id
        c_gq = s.op("pool", lambda g: g.tensor_scalar(out=gqk[:, 0:768], in0=gqk[:, 0:768], scalar1=0.125, scalar2=None,
                                                      op0=ALU.mult), [l_gqk])
        wcast = []
        wst_rel = [None, None]
        for kc in range(8):
            cc = []
            for hf in range(2):
                j = hf
                ld = s.dma("pool", lambda g, kc=kc, j=j, hf=hf: g.dma_start(
                    out=wst[j][:], in_=win_d[kc * 128:(kc + 1) * 128, hf * 2048:(hf + 1) * 2048]),
                    ("w", j), [wst_rel[j]])
                eng = "dve" if hf == 0 else "pool"
                c = s.op(eng, lambda g, kc=kc, j=j, hf=hf: g.tensor_scalar(
                    out=W[:, kc, hf * 2048:(hf + 1) * 2048], in0=wst[j][:], scalar1=wg[:, kc:kc + 1],
                    scalar2=None, op0=ALU.mult), [ld, l_wg])
                wst_rel[j] = c
                cc.append(c)
            wcast.append(cc)
        zero_set = [s.op("pool", lambda g, i=i: g.memset(vdb[i][:], 0.0)) for i in range(2)]
        ones_set = [s.op("pool", lambda g, i=i: g.memset(vdb[i][:, :, 64:65], 1.0), [zero_set[i]]) for i in range(2)]

        xs_rel = [[], []]
        xT_rel = [[], []]
        stat_rel = [[], []]
        ppj_rel = [[], [], []]
        pqT_rel = [[], []]
        vsb_rel = [[], []]
        gsb_rel = [[], []]
        vdb_rel = [[ones_set[0]], [ones_set[1]]]
        tq_rel = [[], []]
        prev = {}
        pj_i = 0
        outs = []
        for b in range(nblocks):
            j = b % 2
            ldx = s.dma("sp", lambda g, b=b, j=j: g.dma_start(out=xs[j][:], in_=x_d[b * 128:(b + 1) * 128, :]),
                        ("x", j), xs_rel[j])
            a_sq = s.op("act", lambda g, j=j: g.activation(out=junk[:], in_=xs[j][:], func=AF.Square,
                                                            accum_out=stat[j][:, 0:1]), [ldx] + stat_rel[j])
            a_ln = s.op("act", lambda g, j=j: g.activation(out=stat[j][:, 1:2], in_=stat[j][:, 0:1], func=AF.Ln,
                                                            scale=1.0 / DM, bias=EPS), [a_sq])
            a_rs = s.op("act", lambda g, j=j: g.activation(out=stat[j][:, 2:3], in_=stat[j][:, 1:2], func=AF.Exp,
                                                            scale=-0.5), [a_ln])
            c_xb = s.op("dve", lambda g, j=j: g.tensor_copy(out=xb[:], in_=xs[j][:]), [ldx] + prev.get("xb", []))
            tps = []
            for kc in range(8):
                tps.append(s.op("pe", lambda g, kc=kc: g.transpose(out=pxT[:, kc * 128:(kc + 1) * 128],
                                                                    in_=xb[:, kc * 128:(kc + 1) * 128], identity=ident[:]),
                                [c_xb, l_id] + prev.get("pxT", [])))
            prev["xb"] = [tps[-1]]
            xs_rel[j] = [a_sq, c_xb]
            c_xT = s.op("dve", lambda g, j=j: g.tensor_copy(out=xT[j][:].rearrange("p a b -> p (a b)"), in_=pxT[:]),
                        [tps[-1]] + xT_rel[j])
            prev["pxT"] = [c_xT]
            evs = {}
            rs_readers = []
            mm_last = None
            for n in range(8):
                pj = pj_i % 3
                pj_i += 1
                for kc in range(8):
                    mm = s.op("pe", lambda g, kc=kc, n=n, pj=pj, j=j: g.matmul(
                        ppj[pj][:], lhsT=xT[j][:, kc, :], rhs=W[:, kc, n * 512:(n + 1) * 512],
                        start=(kc == 0), stop=(kc == 7)), ([c_xT] + wcast[kc] + ppj_rel[pj]) if kc == 0 else wcast[kc])
                mm_last = mm
                rstd = stat[j][:, 2:3]
                rel = []
                if n == 0:
                    e1 = s.op("dve", lambda g, pj=pj, rstd=rstd: g.tensor_scalar(
                        out=qks[:, 0:256], in0=ppj[pj][:, 0:256], scalar1=rstd, scalar2=0.125,
                        op0=ALU.mult, op1=ALU.mult), [mm, a_rs] + prev.get("qks", []))
                    e2 = s.op("act", lambda g, pj=pj, rstd=rstd: g.activation(
                        out=qks[:, 256:512], in_=ppj[pj][:, 256:512], func=AF.Copy, scale=rstd),
                        [mm, a_rs, e1] + prev.get("qks", []))
                    rel = [e1, e2]
                    evs["qks"] = [e1, e2]
                elif n == 1:
                    e1 = s.op("act", lambda g, pj=pj, rstd=rstd, j=j: g.activation(
                        out=vsb[j][:], in_=ppj[pj][:, 0:256], func=AF.Copy, scale=rstd), [mm, a_rs] + vsb_rel[j])
                    e2 = s.op("act", lambda g, pj=pj, rstd=rstd, j=j: g.activation(
                        out=gsb[j][:, 0:256], in_=ppj[pj][:, 256:512], func=AF.Silu, scale=rstd),
                        [mm, a_rs] + gsb_rel[j])
                    rel = [e1, e2]
                    evs["vs"] = [e1]
                    evs["g0"] = e2
                elif n in (2, 3, 4):
                    c0 = (n - 2) * 512
                    eng = "dve" if n != 3 else "act"
                    if eng == "dve":
                        e1 = s.op("dve", lambda g, pj=pj, rstd=rstd, c0=c0: g.tensor_scalar(
                            out=qkd[:, c0:c0 + 512], in0=ppj[pj][:], scalar1=rstd, scalar2=None, op0=ALU.mult),
                            [mm, a_rs] + prev.get("qkd", []))
                    else:
                        e1 = s.op("act", lambda g, pj=pj, rstd=rstd, c0=c0: g.activation(
                            out=qkd[:, c0:c0 + 512], in_=ppj[pj][:], func=AF.Copy, scale=rstd),
                            [mm, a_rs] + prev.get("qkd", []))
                    rel = [e1]
                    evs.setdefault("qkd", []).append(e1)
                elif n == 5:
                    e1 = s.op("act", lambda g, pj=pj, rstd=rstd, j=j: g.activation(
                        out=vdb[j][:, 0:8, 0:64], in_=ppj[pj][:].rearrange("p (h d) -> p h d", d=64),
                        func=AF.Copy, scale=rstd), [mm, a_rs] + vdb_rel[j])
                    rel = [e1]
                    evs["vd"] = [e1]
                elif n == 6:
                    e1 = s.op("act", lambda g, pj=pj, rstd=rstd, j=j: g.activation(
                        out=vdb[j][:, 8:12, 0:64], in_=ppj[pj][:, 0:256].rearrange("p (h d) -> p h d", d=64),
                        func=AF.Copy, scale=rstd), [mm, a_rs] + vdb_rel[j])
                    e2 = s.op("act", lambda g, pj=pj, rstd=rstd, j=j: g.activation(
                        out=gsb[j][:, 256:512], in_=ppj[pj][:, 256:512], func=AF.Silu, scale=rstd),
                        [mm, a_rs, evs["g0"]])
                    rel = [e1, e2]
                    evs["vd"].append(e1)
                    evs["g1"] = e2
                else:
                    e1 = s.op("act", lambda g, pj=pj, rstd=rstd, j=j: g.activation(
                        out=gsb[j][:, 512:1024], in_=ppj[pj][:], func=AF.Silu, scale=rstd),
                        [mm, a_rs, evs["g1"]])
                    rel = [e1]
                    evs["g2"] = e1
                ppj_rel[pj] = rel
                rs_readers += rel
            xT_rel[j] = [mm_last]
            stat_rel[j] = rs_readers
            o1 = s.dma("sp", lambda g, b=b, j=j: g.dma_start(out=vs_d[b * 128:(b + 1) * 128, :], in_=vsb[j][:]),
                       ("o1", j), evs["vs"])
            o2 = s.dma("sp", lambda g, b=b, j=j: g.dma_start(out=gs_d[b * 128:(b + 1) * 128, :], in_=gsb[j][:]),
                       ("o2", j), [evs["g0"], evs["g1"], evs["g2"]])
            o3 = s.dma("sp", lambda g, b=b, j=j: g.dma_start(
                out=vd_d[b * 128:(b + 1) * 128, :], in_=vdb[j][:].rearrange("p h d -> p (h d)")), ("o3", j), evs["vd"])
            vsb_rel[j] = [o1]
            gsb_rel[j] = [o2]
            vdb_rel[j] = [o3]
            outs += [o1, o2, o3]
            pq = b % 2
            t_sb = []
            for i in range(4):
                t_sb.append(s.op("pe", lambda g, i=i, pq=pq: g.transpose(
                    out=pqT[pq][:, i * 128:(i + 1) * 128], in_=qks[:, i * 128:(i + 1) * 128], identity=ident[:]),
                    evs["qks"] + pqT_rel[pq]))
            prev["qks"] = [t_sb[-1]]
            d_sq = s.op("dve", lambda g: g.tensor_tensor(out=sq[:], in0=qkd[:], in1=qkd[:], op=ALU.mult),
                        evs["qkd"] + prev.get("sq", []))
            d_ss = s.op("dve", lambda g: g.tensor_reduce(out=ssh[:], in_=sq[:].rearrange("p (h d) -> p h d", d=64),
                                                         axis=AX.X, op=ALU.add), [d_sq] + prev.get("ssh", []))
            a_l2 = s.op("act", lambda g: g.activation(out=rsh[:], in_=ssh[:], func=AF.Ln, scale=1.0 / 64, bias=EPS),
                        [d_ss] + prev.get("rsh", []))
            a_r2 = s.op("act", lambda g: g.activation(out=rsh[:], in_=rsh[:], func=AF.Exp, scale=-0.5), [a_l2])
            prev["ssh"] = [a_l2]
            d_n1 = s.op("dve", lambda g: g.tensor_tensor(
                out=sq[:].rearrange("p (h d) -> p h d", d=64), in0=qkd[:].rearrange("p (h d) -> p h d", d=64),
                in1=rsh[:].unsqueeze(2).to_broadcast([128, 24, 64]), op=ALU.mult), [a_r2, d_ss])
            prev["rsh"] = [d_n1]
            prev["qkd"] = [d_n1]
            d_n2 = s.op("dve", lambda g: g.tensor_tensor(out=sq[:], in0=sq[:], in1=gqk[:], op=ALU.mult), [d_n1, c_gq])
            sq3 = sq[:].rearrange("p (h d) -> p h d", d=64)
            qkb3 = qkb[:].rearrange("p (h d) -> p h d", d=64)
            cosb = cs[:, b, 0:8].unsqueeze(1).to_broadcast([128, 24, 8])
            sinb = cs[:, b, 8:16].unsqueeze(1).to_broadcast([128, 24, 8])
            pr = prev.get("rt", [])
            r0 = s.op("pool", lambda g, cosb=cosb: g.tensor_tensor(out=rt[0][:], in0=sq3[:, :, 0:8], in1=cosb, op=ALU.mult),
                      [d_n2, l_cs] + pr)
            r1 = s.op("pool", lambda g, sinb=sinb: g.tensor_tensor(out=rt[1][:], in0=sq3[:, :, 8:16], in1=sinb, op=ALU.mult),
                      [d_n2, l_cs] + pr)
            r2 = s.op("pool", lambda g, cosb=cosb: g.tensor_tensor(out=rt[2][:], in0=sq3[:, :, 8:16], in1=cosb, op=ALU.mult),
                      [d_n2] + pr)
            r3 = s.op("pool", lambda g, sinb=sinb: g.tensor_tensor(out=rt[3][:], in0=sq3[:, :, 0:8], in1=sinb, op=ALU.mult),
                      [d_n2] + pr)
            pk = prev.get("qkb", [])
            r4 = s.op("pool", lambda g: g.tensor_tensor(out=qkb3[:, :, 0:8], in0=rt[0][:], in1=rt[1][:], op=ALU.subtract),
                      [r0, r1] + pk)
            r5 = s.op("pool", lambda g: g.tensor_tensor(out=qkb3[:, :, 8:16], in0=rt[2][:], in1=rt[3][:], op=ALU.add),
                      [r2, r3] + pk)
            prev["rt"] = [r4, r5]
            d_cp = s.op("dve", lambda g: g.tensor_copy(out=qkb3[:, :, 16:64], in_=sq3[:, :, 16:64]), [d_n2] + pk)
            prev["sq"] = [d_cp, r0, r1, r2, r3]
            t_d = []
            for i in range(4):
                t_d.append(s.op("pe", lambda g, i=i, pq=pq: g.transpose(
                    out=pqT[pq][:, 512 + i * 128:512 + (i + 1) * 128], in_=qkb[:, i * 128:(i + 1) * 128],
                    identity=ident[:]), [r4, r5, d_cp, t_sb[-1]]))
            bs = slice(b * 128, (b + 1) * 128)
            ev1 = s.op("act", lambda g, pq=pq, j=j: g.copy(
                out=qTs[j][:], in_=pqT[pq][:, 0:256].rearrange("p (a t) -> p a t", t=128)), [t_d[-1]] + tq_rel[j])
            ev2 = s.op("act", lambda g, pq=pq, j=j: g.copy(
                out=kTs[j][:], in_=pqT[pq][:, 256:512].rearrange("p (a t) -> p a t", t=128)), [t_d[-1]] + tq_rel[j])
            ev3 = s.op("dve", lambda g, pq=pq, j=j: g.tensor_copy(
                out=qTd[j][:, 0:4, :], in_=pqT[pq][:, 512:1024].rearrange("p (a t) -> p a t", t=128)),
                [t_d[-1], ev2] + tq_rel[j])
            pqT_rel[pq] = [ev1, ev2, ev3]
            po = 1 - pq
            t_e = []
            for i in range(8):
                t_e.append(s.op("pe", lambda g, i=i, po=po: g.transpose(
                    out=pqT[po][:, i * 128:(i + 1) * 128], in_=qkb[:, (4 + i) * 128:(5 + i) * 128],
                    identity=ident[:]), [r4, r5, d_cp] + pqT_rel[po]))
            prev["qkb"] = [t_e[-1]]
            ev4 = s.op("act", lambda g, po=po, j=j: g.copy(
                out=qTd[j][:, 4:6, :], in_=pqT[po][:, 0:256].rearrange("p (a t) -> p a t", t=128)),
                [t_e[-1]] + tq_rel[j])
            ev5 = s.op("dve", lambda g, po=po, j=j: g.tensor_copy(
                out=kTd[j][:], in_=pqT[po][:, 256:1024].rearrange("p (a t) -> p a t", t=128)), [t_e[-1], ev4] + tq_rel[j])
            pqT_rel[po] = [ev4, ev5]
            f1 = s.dma("sp", lambda g, j=j, bs=bs: g.dma_start(out=qTs_d[:, :, bs], in_=qTs[j][:]), ("f1", j), [ev1])
            f2 = s.dma("sp", lambda g, j=j, bs=bs: g.dma_start(out=kTs_d[:, :, bs], in_=kTs[j][:]), ("f2", j), [ev2])
            f3 = s.dma("sp", lambda g, j=j, bs=bs: g.dma_start(out=qTd_d[:, :, bs], in_=qTd[j][:]), ("f3", j), [ev3, ev4])
            f4 = s.dma("sp", lambda g, j=j, bs=bs: g.dma_start(out=kTd_d[:, :, bs], in_=kTd[j][:]), ("f4", j), [ev5])
            tq_rel[j] = [f1, f2, f3, f4]
            outs += [f1, f2, f3, f4]
        s.emit(final_waits=outs[-14:])


def rope_tables():
    pos = np.arange(SEQ, dtype=np.float32)
    inv = (1.0 / (np.float32(500000.0) ** (np.arange(8, dtype=np.float32) * np.float32(2.0) / np.float32(16)))).astype(np.float32)
    ang = (pos[:, None] * inv[None, :]).astype(np.float32).astype(np.float64)
    return np.cos(ang).astype(np.float32), np.sin(ang).astype(np.float32)


_ROPE = None


def host_P_inputs(norm_g_l, gq_l, gk_l, c):
    global _ROPE
    if _ROPE is None:
        _ROPE = rope_tables()
    co, si = _ROPE
    cs = np.concatenate([co[c * CH:(c + 1) * CH], si[c * CH:(c + 1) * CH]], -1)
    cs = np.ascontiguousarray(cs.reshape(NB, 128, 16).transpose(1, 0, 2))
    wg = np.ascontiguousarray(norm_g_l.reshape(8, 128).T)
    gqk = np.concatenate([np.tile(gq_l, 12), np.tile(gk_l, 12)])[None, :].repeat(128, 0)
    ident = np.eye(128, dtype=np.float32).astype(ml_dtypes.bfloat16)
    return dict(wg=wg.astype(np.float32), gqk=np.ascontiguousarray(gqk.astype(np.float32)), cs=cs.astype(np.float32),
                ident=ident)


def emit_S(nc, qz_d, kT_d, v_d, masks_d, ntri_d, ident_d, oa_d, npairs=8):
    with contextlib.ExitStack() as st:
        T = lambda n, s, d: st.enter_context(nc.sbuf_tensor("S_" + n, s, d))
        PS = lambda n, s, d: st.enter_context(nc.psum_tensor("S_" + n, s, d))
        nkb = 16 * npairs
        kT = T("kT", [128, 2, nkb * 128], BF16)
        V = T("V", [128, nkb, 256], BF16)
        qz = T("qz", [128, 2, 2 * npairs * 128], BF16)
        masks = T("masks", [128, 33, 128], BF16)
        ntri = T("ntri", [128, 128], BF16)
        ident = T("ident", [128, 128], BF16)
        ones = T("ones", [128, 2], BF16)
        ebuf = [T(f"e{i}", [128, 1024], F32) for i in range(2)]
        spb = [T(f"sp{i}", [128, 1024], BF16) for i in range(2)]
        ab = [T(f"a{i}", [128, 1024], BF16) for i in range(2)]
        tmp = [T(f"tmp{i}", [128, 512], F32) for i in range(2)]
        Oacc = T("Oacc", [128, 2 * npairs, 256], F32)
        Rpos = T("Rpos", [128, 2 * npairs, 4], F32)
        expR = [T(f"expR{i}", [128, 2, 4], F32) for i in range(2)]
        zps = [PS(f"z{i}", [128, 1024], F32) for i in range(2)]
        ops_ = [[PS(f"o{i}{a}", [128, 512], F32) for a in range(2)] for i in range(2)]

        s = Sched(nc, "S")
        l_c = s.dma("sp", lambda g: g.dma_start(out=masks[:], in_=masks_d), "c")
        l_c = s.dma("sp", lambda g: g.dma_start(out=ntri[:], in_=ntri_d), "c")
        l_c = s.dma("sp", lambda g: g.dma_start(out=ident[:], in_=ident_d), "c")
        l_c = s.dma("sp", lambda g: g.dma_start(out=qz[:], in_=qz_d[:, :, 0:2 * npairs * 128]), "c")
        i_1 = s.op("pool", lambda g: g.memset(ones[:], 1.0))
        i_2 = s.op("pool", lambda g: g.memset(Oacc[:], 0.0))
        i_3 = s.op("pool", lambda g: g.memset(Rpos[:], 0.0))
        l_kv = []
        for m in range(npairs):
            cs_ = slice(m * 2048, (m + 1) * 2048)
            l1 = s.dma("sp", lambda g, cs_=cs_: g.dma_start(out=kT[:, :, cs_], in_=kT_d[:, :, cs_]), ("kv", m))
            l2 = s.dma("sp", lambda g, m=m, cs_=cs_: g.dma_start(
                out=V[:, m * 16:(m + 1) * 16, :], in_=v_d[cs_, :].rearrange("(n p) c -> p n c", p=128)), ("kv", m))
            l_kv.append(l2)

        steps = [(m, n) for m in range(npairs) for n in range(16 * m + 15, -1, -1)]
        NS = len(steps)
        A1 = [None] * NS; A2 = [None] * NS; A3 = [None] * NS; X = [None] * NS
        QK = [None] * NS; CS = [None] * NS; TT = [None] * NS; AV = [None] * NS
        ACC = [None] * NS; RU = [None] * NS; PA = [None] * NS

        def emit_qk(k):
            m, n = steps[k]
            zb = zps[k % 2]
            deps = [l_c, l_kv[m]] + ([A3[k - 2]] if k >= 2 else [])
            last = None
            for a in range(2):
                slot = 2 * m + a
                for h in range(4):
                    pr = slice((h % 2) * 64, (h % 2) * 64 + 64)
                    col = slice(a * 512 + h * 128, a * 512 + (h + 1) * 128)
                    last = s.op("pe", lambda g, zb=zb, pr=pr, col=col, h=h, n=n, slot=slot: g.matmul(
                        zb[:, col], lhsT=kT[pr, h // 2, n * 128:(n + 1) * 128],
                        rhs=qz[pr, h // 2, slot * 128:(slot + 1) * 128],
                        start=(h == 0), stop=False, skip_group_check=True), deps)
                    deps = []
                    if n >= 16 * m or FORCE_MASK:
                        mi = 2 * (n - 16 * m) + a if n >= 16 * m else 32
                        last = s.op("pe", lambda g, zb=zb, col=col, mi=mi: g.matmul(
                            zb[:, col], lhsT=ident[:], rhs=masks[:, mi, :],
                            start=False, stop=False, skip_group_check=True))
            QK[k] = last

        def emit_pe_mid(k):
            m, n = steps[k]
            zb = zps[k % 2]
            for a in range(2):
                CS[k] = s.op("pe", lambda g, zb=zb, a=a, k=k: g.matmul(
                    zb[:, a * 512:(a + 1) * 512], lhsT=ntri[:], rhs=spb[k % 2][:, a * 512:(a + 1) * 512],
                    start=False, stop=True, skip_group_check=True), [A2[k]])
            deps = [i_1] + ([ACC[k - 2], RU[k - 2]] if k >= 2 else [])
            for a in range(2):
                ob = ops_[k % 2][a]
                for h in range(4):
                    TT[k] = s.op("pe", lambda g, ob=ob, a=a, h=h, k=k: g.matmul(
                        ob[:, 256 + 2 * h:258 + 2 * h], lhsT=spb[k % 2][:, a * 512 + h * 128:a * 512 + (h + 1) * 128],
                        rhs=ones[:], start=(h == 0), stop=False, skip_group_check=True), deps)
                    deps = []

        def emit_av(k):
            m, n = steps[k]
            for a in range(2):
                ob = ops_[k % 2][a]
                for h in range(4):
                    AV[k] = s.op("pe", lambda g, ob=ob, a=a, h=h, k=k, n=n: g.matmul(
                        ob[:, h * 64:(h + 1) * 64], lhsT=ab[k % 2][:, a * 512 + h * 128:a * 512 + (h + 1) * 128],
                        rhs=V[:, n, h * 64:(h + 1) * 64], start=False, stop=(h == 3), skip_group_check=True),
                        [A3[k], TT[k]])

        def emit_a1(k):
            A1[k] = s.op("act", lambda g, k=k: g.activation(out=ebuf[k % 2][:], in_=zps[k % 2][:], func=AF.Exp),
                         [QK[k]] + ([A2[k - 2]] if k >= 2 else []))

        def emit_a2(k):
            A2[k] = s.op("act", lambda g, k=k: g.activation(out=spb[k % 2][:], in_=ebuf[k % 2][:], func=AF.Ln, bias=1.0),
                         [A1[k]] + ([CS[k - 2], TT[k - 2]] if k >= 2 else []))

        def emit_x(k):
            m, n = steps[k]
            deps = [i_3] + ([RU[k - 1]] if k >= 1 else []) + ([ACC[k - 2]] if k >= 2 else [])
            X[k] = s.op("act", lambda g, k=k, m=m: g.activation(out=expR[k % 2][:], in_=Rpos[:, 2 * m:2 * m + 2, :],
                                                                 func=AF.Exp, scale=-1.0), deps)

        def emit_a3(k):
            A3[k] = s.op("act", lambda g, k=k: g.activation(out=ab[k % 2][:], in_=zps[k % 2][:], func=AF.Exp),
                         [CS[k]] + ([AV[k - 2]] if k >= 2 else []))

        def emit_acc(k):
            m, n = steps[k]
            d0 = None
            for a in range(2):
                ob = ops_[k % 2][a]
                d0 = s.op("dve", lambda g, ob=ob, a=a, k=k: g.tensor_tensor(
                    out=tmp[k % 2][:, a * 256:(a + 1) * 256].rearrange("p (h d) -> p h d", d=64),
                    in0=ob[:, 0:256].rearrange("p (h d) -> p h d", d=64),
                    in1=expR[k % 2][:, a, :].unsqueeze(2).to_broadcast([128, 4, 64]), op=ALU.mult),
                    [AV[k], X[k]] + ([PA[k - 2]] if k >= 2 else []))
            ACC[k] = d0
            for a in range(2):
                ob = ops_[k % 2][a]
                RU[k] = s.op("dve", lambda g, ob=ob, a=a, m=m: g.tensor_tensor(
                    out=Rpos[:, 2 * m + a, :], in0=Rpos[:, 2 * m + a, :], in1=ob[:, 256:264].rearrange("p (h t) -> p h t", t=2)[:, :, 0], op=ALU.add),
                    [TT[k], X[k], i_3])
            PA[k] = s.op("pool", lambda g, k=k, m=m: g.tensor_tensor(
                out=Oacc[:, 2 * m:2 * m + 2, :].rearrange("p a c -> p (a c)"),
                in0=Oacc[:, 2 * m:2 * m + 2, :].rearrange("p a c -> p (a c)"), in1=tmp[k % 2][:], op=ALU.add),
                [ACC[k], i_2] + ([PA[k - 1]] if k >= 1 else []))

        outs = []
        emit_qk(0)
        emit_a1(0)
        emit_a2(0)
        if NS > 1:
            emit_qk(1)
        for k in range(NS):
            emit_pe_mid(k)
            if k >= 1:
                emit_av(k - 1)
                emit_acc(k - 1)
            if k + 1 < NS:
                emit_a1(k + 1)
            emit_x(k)
            emit_a3(k)
            if k + 1 < NS:
                emit_a2(k + 1)
            if k + 2 < NS:
                emit_qk(k + 2)
            if k >= 1:
                m1, n1 = steps[k - 1]
                if n1 == 0:
                    outs.append(s.dma("sp", lambda g, m1=m1: g.dma_start(
                        out=oa_d[m1 * 256:(m1 + 1) * 256, :].rearrange("(a p) c -> p a c", p=128),
                        in_=Oacc[:, 2 * m1:2 * m1 + 2, :]), ("o", m1), [PA[k - 1]]))
        emit_av(NS - 1)
        emit_acc(NS - 1)
        m1 = steps[NS - 1][0]
        outs.append(s.dma("sp", lambda g, m1=m1: g.dma_start(
            out=oa_d[m1 * 256:(m1 + 1) * 256, :].rearrange("(a p) c -> p a c", p=128),
            in_=Oacc[:, 2 * m1:2 * m1 + 2, :]), ("o", m1), [PA[NS - 1]]))
        s.emit(final_waits=outs)


def host_S_consts(c):
    sidx = np.arange(128)[:, None]
    tidx = np.arange(128)[None, :]
    diag = np.where(sidx >= tidx, NEG, 0.0).astype(np.float32)
    full = np.full((128, 128), NEG, np.float32)
    zero = np.zeros((128, 128), np.float32)
    mk = np.zeros((128, 33, 128), np.float32)
    for j in range(16):
        for a, cc in enumerate((c, 15 - c)):
            mk[:, 2 * j + a, :] = zero if j < cc else (diag if j == cc else full)
    ntri = -(sidx >= tidx).astype(np.float32)
    return dict(masks=mk.astype(ml_dtypes.bfloat16), ntri=ntri.astype(ml_dtypes.bfloat16))


def dil_masks():
    jk = np.arange(128)[:, None]
    iq = np.arange(128)[None, :]
    hi = np.zeros((128, 19, 128), np.float32)
    for dl in range(17):
        d = 128 * dl + iq - jk
        w = ((d >= 0) & (d <= 128)).astype(np.float64) + ((d >= 0) & (d % 4 == 0) & (d <= 512)) \
            + ((d >= 0) & (d % 16 == 0) & (d <= 2048))
        lw = np.where(w > 0, np.log(np.maximum(w, 1e-30)), NEG)
        h_ = lw.astype(np.float32).astype(ml_dtypes.bfloat16).astype(np.float32)
        hi[:, dl, :] = h_
        if dl < 2:
            hi[:, 17 + dl, :] = np.where(w > 0, lw - h_, 0.0)
    return hi.astype(ml_dtypes.bfloat16)


def emit_D(nc, qT_d, kT_d, v_d, masks_d, ident_d, od_d, nblocks=NB, nheads=12):
    with contextlib.ExitStack() as st:
        T = lambda n, s, d: st.enter_context(nc.sbuf_tensor("D_" + n, s, d))
        PS = lambda n, s, d: st.enter_context(nc.psum_tensor("D_" + n, s, d))
        qT = T("qT", [128, 6, CH], BF16)
        kT = T("kT", [128, 6, 2 * CH], BF16)
        V = T("V", [128, 32, 960], BF16)
        masks = T("masks", [128, 19, 128], BF16)
        ident = T("ident", [128, 128], BF16)
        pb = [T(f"p{i}", [128, 512], BF16) for i in range(3)]
        rec = [T(f"rec{i}", [128, 12], F32) for i in range(2)]
        odb = [T(f"od{i}", [128, 768], F32) for i in range(2)]
        sps = [PS(f"s{i}", [128, 512], F32) for i in range(3)]
        ops_ = [PS(f"o{i}", [128, 2, 512], F32) for i in range(2)]

        s = Sched(nc, "D")
        l_c = s.dma("sp", lambda g: g.dma_start(out=masks[:], in_=masks_d), "c")
        l_c = s.dma("sp", lambda g: g.dma_start(out=ident[:], in_=ident_d), "c")
        l_c = s.dma("sp", lambda g: g.dma_start(out=qT[:], in_=qT_d), "c")
        l_c = s.dma("sp", lambda g: g.dma_start(out=kT[:], in_=kT_d), "c")
        for i in range(4):
            l_c = s.dma("sp", lambda g, i=i: g.dma_start(
                out=V[:, i * 8:(i + 1) * 8, :], in_=v_d[i * 1024:(i + 1) * 1024, :].rearrange("(n p) c -> p n c", p=128)), "c")

        grp_i = 0
        s_rel = [[], [], []]
        p_rel = [[], [], []]
        o_rel = [[], []]
        od_rel = [[], []]
        outs = []
        for b in range(nblocks):
            ob = ops_[b % 2]
            last_av = None
            first_in_bank = [True, True]
            for h in range(nheads):
                pr = slice((h % 2) * 64, (h % 2) * 64 + 64)
                g6 = h // 2
                bank = h // 6
                ocol = (h % 6) * 80
                for g0 in range(0, 17, 4):
                    dls = list(range(g0, min(g0 + 4, 17)))
                    r = grp_i % 3
                    grp_i += 1
                    deps = [l_c] + s_rel[r]
                    mm = None
                    for ti, dl in enumerate(dls):
                        kb = 16 + b - dl
                        col = slice(ti * 128, (ti + 1) * 128)
                        mm = s.op("pe", lambda g, r=r, col=col, pr=pr, g6=g6, kb=kb, b=b, ti=ti: g.matmul(
                            sps[r][:, col], lhsT=kT[pr, g6, kb * 128:(kb + 1) * 128], rhs=qT[pr, g6, b * 128:(b + 1) * 128],
                            start=(ti == 0), stop=False, skip_group_check=True), deps)
                        deps = []
                        mm = s.op("pe", lambda g, r=r, col=col, dl=dl: g.matmul(
                            sps[r][:, col], lhsT=ident[:], rhs=masks[:, dl, :], start=False, stop=(dl >= 2),
                            skip_group_check=True))
                        if dl < 2:
                            mm = s.op("pe", lambda g, r=r, col=col, dl=dl: g.matmul(
                                sps[r][:, col], lhsT=ident[:], rhs=masks[:, 17 + dl, :], start=False, stop=True,
                                skip_group_check=True))
                    w = len(dls) * 128
                    ex = s.op("act", lambda g, r=r, w=w: g.activation(out=pb[r][:, 0:w], in_=sps[r][:, 0:w], func=AF.Exp),
                              [mm] + p_rel[r])
                    s_rel[r] = [ex]
                    for ti, dl in enumerate(dls):
                        kb = 16 + b - dl
                        deps = [ex] + (o_rel[b % 2] if first_in_bank[bank] else [])
                        last_av = s.op("pe", lambda g, ob=ob, bank=bank, ocol=ocol, r=r, ti=ti, kb=kb, h=h,
                                       fb=first_in_bank[bank], dl=dl: g.matmul(
                            ob[:, bank, ocol:ocol + 80], lhsT=pb[r][:, ti * 128:(ti + 1) * 128],
                            rhs=V[:, kb, h * 80:(h + 1) * 80], start=fb, stop=(dl == 16), skip_group_check=True), deps)
                        first_in_bank[bank] = False
                    p_rel[r] = [last_av]
            j = b % 2
            o4 = ob[:, :, 0:480].rearrange("p a (h d) -> p a h d", d=80)
            d1 = s.op("dve", lambda g, o4=o4, j=j: g.reciprocal(
                out=rec[j][:].rearrange("p (a h) -> p a h", a=2), in_=o4[:, :, :, 64]), [last_av] + od_rel[j])
            d2 = None
            for a in range(2):
                d2 = s.op("dve", lambda g, o4=o4, j=j, a=a: g.tensor_tensor(
                    out=odb[j][:, a * 384:(a + 1) * 384].rearrange("p (h d) -> p h d", d=64),
                    in0=o4[:, a, :, 0:64], in1=rec[j][:, a * 6:(a + 1) * 6].unsqueeze(2).to_broadcast([128, 6, 64]),
                    op=ALU.mult), [d1])
            o_rel[b % 2] = [d2]
            od_ = s.dma("sp", lambda g, b=b, j=j: g.dma_start(out=od_d[b * 128:(b + 1) * 128, :], in_=odb[j][:]),
                        ("o", j), [d2])
            od_rel[j] = [od_]
            outs.append(od_)
        s.emit(final_waits=outs[-2:])


def emit_O(nc, x_d, oa_d, od_d, gs_d, wo_d, ident_d, xo_d, nblocks=NB):
    with contextlib.ExitStack() as st:
        T = lambda n, s, d: st.enter_context(nc.sbuf_tensor("O_" + n, s, d))
        PS = lambda n, s, d: st.enter_context(nc.psum_tensor("O_" + n, s, d))
        Wo = T("Wo", [128, 8, 1024], BF16)
        wst = [T(f"wst{i}", [128, 1024], F32) for i in range(2)]
        ident = T("ident", [128, 128], BF16)
        xs = [T(f"xs{i}", [128, 1024], F32) for i in range(2)]
        att = [T(f"att{i}", [128, 1024], F32) for i in range(2)]
        gsb = [T(f"gs{i}", [128, 1024], F32) for i in range(2)]
        mix = T("mix", [128, 1024], BF16)
        mixT = T("mixT", [128, 8, 128], BF16)
        xo = [T(f"xo{i}", [128, 1024], F32) for i in range(2)]
        pT = PS("pT", [128, 1024], BF16)
        py = [PS(f"py{i}", [128, 512], F32) for i in range(2)]

        s = Sched(nc, "O")
        l_id = s.dma("sp", lambda g: g.dma_start(out=ident[:], in_=ident_d), "c")
        wrel = [None, None]
        wc = []
        for kc in range(8):
            j = kc % 2
            ld = s.dma("pool", lambda g, kc=kc, j=j: g.dma_start(out=wst[j][:], in_=wo_d[kc * 128:(kc + 1) * 128, :]),
                       ("w", j), [wrel[j]])
            c_ = s.op("pool", lambda g, kc=kc, j=j: g.tensor_copy(out=Wo[:, kc, :], in_=wst[j][:]), [ld])
            wrel[j] = c_
            wc.append(c_)
        in_rel = [[], []]
        xo_rel = [[], []]
        prev = {}
        outs = []
        for b in range(nblocks):
            j = b % 2
            rs = slice(b * 128, (b + 1) * 128)
            l1 = s.dma("sp", lambda g, j=j, rs=rs: g.dma_start(out=xs[j][:], in_=x_d[rs, :]), ("i", j), in_rel[j])
            l2 = s.dma("sp", lambda g, j=j, rs=rs: g.dma_start(out=att[j][:, 0:256], in_=oa_d[rs, :]), ("i", j))
            l3 = s.dma("sp", lambda g, j=j, rs=rs: g.dma_start(out=att[j][:, 256:1024], in_=od_d[rs, :]), ("i", j))
            l4 = s.dma("sp", lambda g, j=j, rs=rs: g.dma_start(out=gsb[j][:], in_=gs_d[rs, :]), ("i", j))
            d1 = s.op("dve", lambda g, j=j: g.tensor_tensor(out=mix[:], in0=att[j][:], in1=gsb[j][:], op=ALU.mult),
                      [l4] + prev.get("mix", []))
            tp = None
            for kc in range(8):
                tp = s.op("pe", lambda g, kc=kc: g.transpose(out=pT[:, kc * 128:(kc + 1) * 128],
                                                             in_=mix[:, kc * 128:(kc + 1) * 128], identity=ident[:]),
                          [d1, l_id] + prev.get("pT", []))
            prev["mix"] = [tp]
            c1 = s.op("act", lambda g: g.copy(out=mixT[:].rearrange("p a t -> p (a t)"), in_=pT[:]),
                      [tp] + prev.get("mixT", []))
            prev["pT"] = [c1]
            ev = []
            mm = None
            for hf in range(2):
                for kc in range(8):
                    mm = s.op("pe", lambda g, hf=hf, kc=kc: g.matmul(
                        py[hf][:], lhsT=mixT[:, kc, :], rhs=Wo[:, kc, hf * 512:(hf + 1) * 512],
                        start=(kc == 0), stop=(kc == 7)), ([c1, wc[kc]] + prev.get(("py", hf), [])) if kc == 0 else [wc[kc]])
                e_ = s.op("dve", lambda g, hf=hf, j=j: g.tensor_tensor(
                    out=xo[j][:, hf * 512:(hf + 1) * 512], in0=xs[j][:, hf * 512:(hf + 1) * 512], in1=py[hf][:],
                    op=ALU.add), [mm, l4] + xo_rel[j])
                prev[("py", hf)] = [e_]
                ev.append(e_)
            prev["mixT"] = [mm]
            in_rel[j] = [d1, ev[1]]
            o_ = s.dma("sp", lambda g, j=j, rs=rs: g.dma_start(out=xo_d[rs, :], in_=xo[j][:]), ("o", j), ev)
            xo_rel[j] = [o_]
            outs.append(o_)
        s.emit(final_waits=outs[-2:])


_PROG = {}
_DBG = None


def _dram(nc, n, s, d, k):
    return nc.dram_tensor(n, s, d, kind=k).ap()


def build_P():
    nc = bass.Bass("TRN2", target_bir_lowering=False)
    I, O = "ExternalInput", "ExternalOutput"
    x_d = _dram(nc, "x", [CH, 1024], F32, I)
    wg_d = _dram(nc, "wg", [128, 8], F32, I)
    win_d = _dram(nc, "win", [1024, 4096], F32, I)
    gqk_d = _dram(nc, "gqk", [128, 1536], F32, I)
    cs_d = _dram(nc, "cs", [128, NB, 16], F32, I)
    ident_d = _dram(nc, "ident", [128, 128], BF16, I)
    qTs_d = _dram(nc, "qTs", [128, 2, CH], BF16, O)
    kTs_d = _dram(nc, "kTs", [128, 2, CH], BF16, O)
    vs_d = _dram(nc, "vs", [CH, 256], BF16, O)
    qTd_d = _dram(nc, "qTd", [128, 6, CH], BF16, O)
    kTd_d = _dram(nc, "kTd", [128, 6, CH], BF16, O)
    vd_d = _dram(nc, "vd", [CH, 960], BF16, O)
    gs_d = _dram(nc, "gs", [CH, 1024], F32, O)
    emit_P(nc, None, x_d, wg_d, win_d, gqk_d, cs_d, ident_d, qTs_d, kTs_d, vs_d, qTd_d, kTd_d, vd_d, gs_d)
    return nc


def build_SD():
    nc = bass.Bass("TRN2", target_bir_lowering=False)
    I, O = "ExternalInput", "ExternalOutput"
    qz_d = _dram(nc, "qz", [128, 2, 2048], BF16, I)
    kT_d = _dram(nc, "kTs", [128, 2, SEQ], BF16, I)
    v_d = _dram(nc, "vs", [SEQ, 256], BF16, I)
    masks_d = _dram(nc, "masks", [128, 33, 128], BF16, I)
    ntri_d = _dram(nc, "ntri", [128, 128], BF16, I)
    ident_d = _dram(nc, "ident", [128, 128], BF16, I)
    oa_d = _dram(nc, "oa", [2048, 256], F32, O)
    qTd_d = _dram(nc, "qTd", [128, 6, CH], BF16, I)
    kTd_d = _dram(nc, "kTd", [128, 6, 2 * CH], BF16, I)
    vd_d = _dram(nc, "vd", [2 * CH, 960], BF16, I)
    dmask_d = _dram(nc, "dmask", [128, 19, 128], BF16, I)
    od_d = _dram(nc, "od", [CH, 768], F32, O)
    emit_S(nc, qz_d, kT_d, v_d, masks_d, ntri_d, ident_d, oa_d)
    emit_D(nc, qTd_d, kTd_d, vd_d, dmask_d, ident_d, od_d)
    return nc


def build_O():
    nc = bass.Bass("TRN2", target_bir_lowering=False)
    I, O = "ExternalInput", "ExternalOutput"
    x_d = _dram(nc, "x", [CH, 1024], F32, I)
    oa_d = _dram(nc, "oa", [CH, 256], F32, I)
    od_d = _dram(nc, "od", [CH, 768], F32, I)
    gs_d = _dram(nc, "gs", [CH, 1024], F32, I)
    wo_d = _dram(nc, "wo", [1024, 1024], F32, I)
    ident_d = _dram(nc, "ident", [128, 128], BF16, I)
    xo_d = _dram(nc, "xo", [CH, 1024], F32, O)
    emit_O(nc, x_d, oa_d, od_d, gs_d, wo_d, ident_d, xo_d)
    return nc


def _slots(c):
    out = []
    for m in range(8):
        out += [16 * m + c, 16 * m + 15 - c]
    return out


def kernel(x, norm_g, w_in, q_norm_g, k_norm_g, w_out):
    x = np.asarray(x, np.float32)
    norm_g = np.asarray(norm_g, np.float32)
    w_in = np.asarray(w_in, np.float32)
    q_norm_g = np.asarray(q_norm_g, np.float32)
    k_norm_g = np.asarray(k_norm_g, np.float32)
    w_out = np.asarray(w_out, np.float32)
    cores = list(range(NCORES))
    bf = ml_dtypes.bfloat16
    ident = np.eye(128, dtype=np.float32).astype(bf)
    dmask = dil_masks()
    sconst = [host_S_consts(c) for c in cores]
    xs = [np.ascontiguousarray(x[0, c * CH:(c + 1) * CH]) for c in cores]
    for l in range(DEPTH):
        ncP = build_P()
        maps = []
        for c in cores:
            hp = host_P_inputs(norm_g[l], q_norm_g[l], k_norm_g[l], c)
            maps.append(dict(x=xs[c], win=w_in[l], **hp))
        rP = run_bass_kernel_spmd(ncP, maps, core_ids=cores).results
        kTs_all = np.ascontiguousarray(np.concatenate([rP[c]["kTs"] for c in cores], axis=2))
        qTs_all = np.concatenate([rP[c]["qTs"] for c in cores], axis=2)
        vs_all = np.ascontiguousarray(np.concatenate([rP[c]["vs"] for c in cores], axis=0))
        ncSD = build_SD()
        maps = []
        for c in cores:
            qz = np.ascontiguousarray(np.concatenate(
                [qTs_all[:, :, b * 128:(b + 1) * 128] for b in _slots(c)], axis=2))
            if c == 0:
                kprev = np.zeros_like(rP[0]["kTd"])
                vprev = np.zeros_like(rP[0]["vd"])
            else:
                kprev, vprev = rP[c - 1]["kTd"], rP[c - 1]["vd"]
            maps.append(dict(qz=qz, kTs=kTs_all, vs=vs_all, masks=sconst[c]["masks"], ntri=sconst[c]["ntri"],
                             ident=ident, qTd=rP[c]["qTd"],
                             kTd=np.ascontiguousarray(np.concatenate([kprev, rP[c]["kTd"]], axis=2)),
                             vd=np.ascontiguousarray(np.concatenate([vprev, rP[c]["vd"]], axis=0)), dmask=dmask))
        rSD = run_bass_kernel_spmd(ncSD, maps, core_ids=cores).results
        oa_all = np.zeros((SEQ, 256), np.float32)
        for c in cores:
            for si, b in enumerate(_slots(c)):
                oa_all[b * 128:(b + 1) * 128] = rSD[c]["oa"][si * 128:(si + 1) * 128]
        ncO = build_O()
        maps = []
        for c in cores:
            maps.append(dict(x=xs[c], oa=np.ascontiguousarray(oa_all[c * CH:(c + 1) * CH]), od=rSD[c]["od"],
                             gs=rP[c]["gs"], wo=w_out[l], ident=ident))
        rO = run_bass_kernel_spmd(ncO, maps, core_ids=cores).results
        xs = [rO[c]["xo"] for c in cores]
        if _DBG is not None:
            _DBG.append(dict(x=np.concatenate(xs, 0), oa=oa_all, od=np.concatenate([rSD[c]["od"] for c in cores], 0),
                             gs=np.concatenate([rP[c]["gs"] for c in cores], 0)))
    return np.concatenate(xs, axis=0)[None].astype(np.float32)
```

```python
import contextlib
import numpy as np
import ml_dtypes
import concourse.bass as bass
import concourse.mybir as mybir
from concourse.bass_utils import run_bass_kernel_spmd

F32 = mybir.dt.float32
BF16 = mybir.dt.bfloat16
I32 = mybir.dt.int32
AF = mybir.ActivationFunctionType
ALU = mybir.AluOpType
AX = mybir.AxisListType

NCORES = 8
SEQ = 16384
DM = 1024
DEPTH = 4
CH = SEQ // NCORES
NB = CH // 128
NBLK = SEQ // 128
EPS = 1e-6
NEG = -30000.0
NIDX = 70
FORCE_MASK = False
import os as _os
SHARED_DMA_SEMS = False

ENGS = ("pe", "act", "dve", "pool", "sp")


class Op:
    __slots__ = ("eng", "fn", "deps", "sig", "dma_sem", "has_dependents", "inc")

    def __init__(self, eng, fn, deps, dma_sem=None, inc=16):
        self.inc = inc
        self.eng = eng
        self.fn = fn
        self.deps = [d for d in deps if d is not None]
        self.dma_sem = dma_sem
        self.sig = None
        self.has_dependents = False


class SemPool:
    def __init__(self, nc):
        self.nc = nc
        self.eng_sets = {}
        self.dma = []
        self.dma_sw = []
        self.dma_cc = []

    def engset(self, key):
        if key not in self.eng_sets:
            self.eng_sets[key] = {e: [self.nc.alloc_semaphore(f"se_{key}_{e}"), 0] for e in ENGS}
        return self.eng_sets[key]

    def dmasem(self, i, sw=False):
        lst = {False: self.dma, True: self.dma_sw, "cc": self.dma_cc}[sw]
        while len(lst) <= i:
            lst.append([self.nc.alloc_semaphore(f"sd{sw}_{len(lst)}"), 0])
        return lst[i]


_POOL = [None]


class Sched:
    def __init__(self, nc, name="s", engset=None):
        self.nc = nc
        self.name = name
        self.engset = engset if engset is not None else "misc"
        self.ops = {e: [] for e in ENGS}

    def op(self, eng, fn, deps=()):
        o = Op(eng, fn, deps)
        for d in o.deps:
            d.has_dependents = True
        self.ops[eng].append(o)
        return o

    def dma(self, eng, fn, sem_key, deps=(), inc=16):
        o = Op(eng, fn, deps, dma_sem=sem_key, inc=inc)
        for d in o.deps:
            d.has_dependents = True
        self.ops[eng].append(o)
        return o

    def emit(self, final_waits=()):
        nc = self.nc
        if _POOL[0] is None or _POOL[0].nc is not nc:
            _POOL[0] = SemPool(nc)
        pool = _POOL[0]
        for o in final_waits:
            o.has_dependents = True
        es = pool.engset(self.engset)
        keys, keys_sw, keys_cc = [], [], []
        for e in ENGS:
            for o in self.ops[e]:
                if o.dma_sem is not None:
                    lst = keys if SHARED_DMA_SEMS else (keys_cc if o.inc == 1 else (keys_sw if e == "pool" else keys))
                    if o.dma_sem not in lst:
                        lst.append(o.dma_sem)
        dsem = {k: pool.dmasem(i) for i, k in enumerate(keys)}
        dsem.update({k: pool.dmasem(i, True) for i, k in enumerate(keys_sw)})
        dsem.update({k: pool.dmasem(i, "cc") for i, k in enumerate(keys_cc)})
        for e in ENGS:
            for o in self.ops[e]:
                if o.dma_sem is not None:
                    ent = dsem[o.dma_sem]
                    ent[1] += o.inc
                    o.sig = (ent[0], ent[1])
                elif o.has_dependents:
                    es[e][1] += 1
                    o.sig = (es[e][0], es[e][1])
        with nc.Block() as block:

            def run(e, eng):
                waited = {}
                for o in self.ops[e]:
                    for d in o.deps:
                        sem, val = d.sig
                        if waited.get(id(sem), 0) < val:
                            eng.wait_ge(sem, val)
                            waited[id(sem)] = val
                    ins = o.fn(eng)
                    if o.sig is not None:
                        ins.then_inc(o.sig[0], o.inc if o.dma_sem is not None else 1)
                if e == "sp":
                    for o in final_waits:
                        sem, val = o.sig
                        if waited.get(id(sem), 0) < val:
                            eng.wait_ge(sem, val)
                            waited[id(sem)] = val

            @block.tensor
            def _(eng):
                run("pe", eng)

            @block.scalar
            def _(eng):
                run("act", eng)

            @block.vector
            def _(eng):
                run("dve", eng)

            @block.gpsimd
            def _(eng):
                run("pool", eng)

            @block.sync
            def _(eng):
                run("sp", eng)


class Ring:
    def __init__(self, bufs):
        self.bufs = bufs
        self.i = 0
        self.readers = [[] for _ in bufs]

    def next(self):
        j = self.i % len(self.bufs)
        self.i += 1
        deps = self.readers[j]
        self.readers[j] = []
        return j, self.bufs[j], deps

    def release(self, j, ops):
        self.readers[j].extend(ops)


def emit_P(nc, st0, x_d, wg_d, win_d, gqk_d, cs_d, ident_d,
           qTs_d, kTs_d, vs_d, qTd_d, kTd_d, vd_d, gs_d, nblocks=NB):
    with contextlib.ExitStack() as st:
        u_ = "P" + _uid() + "_"
        eset_ = "misc"
        T = lambda n, s, d: st.enter_context(nc.sbuf_tensor(u_ + n, s, d))
        PS = lambda n, s, d: st.enter_context(nc.psum_tensor(u_ + n, s, d))
        W = T("W", [128, 8, 4096], BF16)
        wst = [T(f"wst{i}", [128, 2048], F32) for i in range(2)]
        wg = T("wg", [128, 8], F32)
        gqk = T("gqk", [128, 1536], F32)
        cs = T("cs", [128, NB, 16], F32)
        ident = T("ident", [128, 128], BF16)
        xs = [T(f"xs{i}", [128, 1024], F32) for i in range(2)]
        junk = T("junk", [128, 1024], BF16)
        xb = T("xb", [128, 1024], BF16)
        xT = [T(f"xT{i}", [128, 8, 128], BF16) for i in range(2)]
        stat = [T(f"stat{i}", [128, 4], F32) for i in range(2)]
        qks = T("qks", [128, 512], BF16)
        vsb = [T(f"vsb{i}", [128, 256], BF16) for i in range(2)]
        gsb = [T(f"gsb{i}", [128, 1024], F32) for i in range(2)]
        qkd = T("qkd", [128, 1536], F32)
        sq = T("sq", [128, 1536], F32)
        ssh = T("ssh", [128, 24], F32)
        rsh = T("rsh", [128, 24], F32)
        rt = [T(f"rt{i}", [128, 24, 8], F32) for i in range(4)]
        qkb = T("qkb", [128, 1536], BF16)
        vdb = [T(f"vdb{i}", [128, 12, 80], BF16) for i in range(2)]
        qTs = [T(f"qTs{i}", [128, 2, 128], BF16) for i in range(2)]
        kTs = [T(f"kTs{i}", [128, 2, 128], BF16) for i in range(2)]
        qTd = [T(f"qTd{i}", [128, 12, 128], BF16) for i in range(2)]
        kTd = [T(f"kTd{i}", [128, 6, 128], BF16) for i in range(2)]
        pxT = PS("pxT", [128, 1024], BF16)
        pqT = [PS(f"pqT{i}", [128, 1024], BF16) for i in range(2)]
        ppj = [PS(f"ppj{i}", [128, 512], F32) for i in range(3)]

        s = Sched(nc, u_, eset_)
        l_wg = s.dma("sp", lambda g: g.dma_start(out=wg[:], in_=wg_d), "c0")
        l_gqk = s.dma("sp", lambda g: g.dma_start(out=gqk[:], in_=gqk_d), "c0")
        l_cs = s.dma("sp", lambda g: g.dma_start(out=cs[:], in_=cs_d), "c0")
        l_id = s.dma("sp", lambda g: g.dma_start(out=ident[:], in_=ident_d), "c0")
        l_wg = l_gqk = l_cs = l_id
        c_gq = s.op("pool", lambda g: g.tensor_scalar(out=gqk[:, 0:768], in0=gqk[:, 0:768], scalar1=0.125, scalar2=None,
                                                      op0=ALU.mult), [l_gqk])
        wcast = []
        wst_rel = [None, None]
        for kc in range(8):
            cc = []
            for hf in range(2):
                j = hf
                ld = s.dma("pool", lambda g, kc=kc, j=j, hf=hf: g.dma_start(
                    out=wst[j][:], in_=win_d[kc * 128:(kc + 1) * 128, hf * 2048:(hf + 1) * 2048]),
                    ("w", j), [wst_rel[j]])
                eng = "dve" if hf == 0 else "pool"
                c = s.op(eng, lambda g, kc=kc, j=j, hf=hf: g.tensor_scalar(
                    out=W[:, kc, hf * 2048:(hf + 1) * 2048], in0=wst[j][:], scalar1=wg[:, kc:kc + 1],
                    scalar2=None, op0=ALU.mult), [ld, l_wg])
                wst_rel[j] = c
                cc.append(c)
            wcast.append(cc)
        zero_set = [s.op("pool", lambda g, i=i: g.memset(vdb[i][:], 0.0)) for i in range(2)]
        zq_set = [s.op("pool", lambda g, i=i: g.memset(qTd[i][:], 0.0)) for i in range(2)]
        ones_set = [s.op("pool", lambda g, i=i: g.memset(vdb[i][:, :, 64:65], 1.0), [zero_set[i]]) for i in range(2)]

        xs_rel = [[], []]
        xT_rel = [[], []]
        stat_rel = [[], []]
        ppj_rel = [[], [], []]
        pqT_rel = [[], []]
        vsb_rel = [[], []]
        gsb_rel = [[], []]
        vdb_rel = [[ones_set[0]], [ones_set[1]]]
        tq_rel = [[zq_set[0]], [zq_set[1]]]
        prev = {}
        pj_i = 0
        outs = []
        for b in range(nblocks):
            j = b % 2
            ldx = s.dma("sp", lambda g, b=b, j=j: g.dma_start(out=xs[j][:], in_=x_d[b * 128:(b + 1) * 128, :]),
                        ("x", j), xs_rel[j])
            a_sq = s.op("act", lambda g, j=j: g.activation(out=junk[:], in_=xs[j][:], func=AF.Square,
                                                            accum_out=stat[j][:, 0:1]), [ldx] + stat_rel[j])
            a_ln = s.op("act", lambda g, j=j: g.activation(out=stat[j][:, 1:2], in_=stat[j][:, 0:1], func=AF.Ln,
                                                            scale=1.0 / DM, bias=EPS), [a_sq])
            a_rs = s.op("act", lambda g, j=j: g.activation(out=stat[j][:, 2:3], in_=stat[j][:, 1:2], func=AF.Exp,
                                                            scale=-0.5), [a_ln])
            c_xb = s.op("dve", lambda g, j=j: g.tensor_copy(out=xb[:], in_=xs[j][:]), [ldx] + prev.get("xb", []))
            tps = []
            for kc in range(8):
                tps.append(s.op("pe", lambda g, kc=kc: g.transpose(out=pxT[:, kc * 128:(kc + 1) * 128],
                                                                    in_=xb[:, kc * 128:(kc + 1) * 128], identity=ident[:]),
                                [c_xb, l_id] + prev.get("pxT", [])))
            prev["xb"] = [tps[-1]]
            xs_rel[j] = [a_sq, c_xb]
            c_xT = s.op("dve", lambda g, j=j: g.tensor_copy(out=xT[j][:].rearrange("p a b -> p (a b)"), in_=pxT[:]),
                        [tps[-1]] + xT_rel[j])
            prev["pxT"] = [c_xT]
            evs = {}
            rs_readers = []
            mm_last = None
            for n in range(8):
                pj = pj_i % 3
                pj_i += 1
                for kc in range(8):
                    mm = s.op("pe", lambda g, kc=kc, n=n, pj=pj, j=j: g.matmul(
                        ppj[pj][:], lhsT=xT[j][:, kc, :], rhs=W[:, kc, n * 512:(n + 1) * 512],
                        start=(kc == 0), stop=(kc == 7)), ([c_xT] + wcast[kc] + ppj_rel[pj]) if kc == 0 else wcast[kc])
                mm_last = mm
                rstd = stat[j][:, 2:3]
                rel = []
                if n == 0:
                    e1 = s.op("dve", lambda g, pj=pj, rstd=rstd: g.tensor_scalar(
                        out=qks[:, 0:256], in0=ppj[pj][:, 0:256], scalar1=rstd, scalar2=0.125,
                        op0=ALU.mult, op1=ALU.mult), [mm, a_rs] + prev.get("qks", []))
                    e2 = s.op("act", lambda g, pj=pj, rstd=rstd: g.activation(
                        out=qks[:, 256:512], in_=ppj[pj][:, 256:512], func=AF.Copy, scale=rstd),
                        [mm, a_rs, e1] + prev.get("qks", []))
                    rel = [e1, e2]
                    evs["qks"] = [e1, e2]
                elif n == 1:
                    e1 = s.op("act", lambda g, pj=pj, rstd=rstd, j=j: g.activation(
                        out=vsb[j][:], in_=ppj[pj][:, 0:256], func=AF.Copy, scale=rstd), [mm, a_rs] + vsb_rel[j])
                    e2 = s.op("act", lambda g, pj=pj, rstd=rstd, j=j: g.activation(
                        out=gsb[j][:, 0:256], in_=ppj[pj][:, 256:512], func=AF.Silu, scale=rstd),
                        [mm, a_rs] + gsb_rel[j])
                    rel = [e1, e2]
                    evs["vs"] = [e1]
                    evs["g0"] = e2
                elif n in (2, 3, 4):
                    c0 = (n - 2) * 512
                    eng = "dve" if n != 3 else "act"
                    if eng == "dve":
                        e1 = s.op("dve", lambda g, pj=pj, rstd=rstd, c0=c0: g.tensor_scalar(
                            out=qkd[:, c0:c0 + 512], in0=ppj[pj][:], scalar1=rstd, scalar2=None, op0=ALU.mult),
                            [mm, a_rs] + prev.get("qkd", []))
                    else:
                        e1 = s.op("act", lambda g, pj=pj, rstd=rstd, c0=c0: g.activation(
                            out=qkd[:, c0:c0 + 512], in_=ppj[pj][:], func=AF.Copy, scale=rstd),
                            [mm, a_rs] + prev.get("qkd", []))
                    rel = [e1]
                    evs.setdefault("qkd", []).append(e1)
                elif n == 5:
                    e1 = s.op("act", lambda g, pj=pj, rstd=rstd, j=j: g.activation(
                        out=vdb[j][:, 0:8, 0:64], in_=ppj[pj][:].rearrange("p (h d) -> p h d", d=64),
                        func=AF.Copy, scale=rstd), [mm, a_rs] + vdb_rel[j])
                    rel = [e1]
                    evs["vd"] = [e1]
                elif n == 6:
                    e1 = s.op("act", lambda g, pj=pj, rstd=rstd, j=j: g.activation(
                        out=vdb[j][:, 8:12, 0:64], in_=ppj[pj][:, 0:256].rearrange("p (h d) -> p h d", d=64),
                        func=AF.Copy, scale=rstd), [mm, a_rs] + vdb_rel[j])
                    e2 = s.op("act", lambda g, pj=pj, rstd=rstd, j=j: g.activation(
                        out=gsb[j][:, 256:512], in_=ppj[pj][:, 256:512], func=AF.Silu, scale=rstd),
                        [mm, a_rs, evs["g0"]])
                    rel = [e1, e2]
                    evs["vd"].append(e1)
                    evs["g1"] = e2
                else:
                    e1 = s.op("act", lambda g, pj=pj, rstd=rstd, j=j: g.activation(
                        out=gsb[j][:, 512:1024], in_=ppj[pj][:], func=AF.Silu, scale=rstd),
                        [mm, a_rs, evs["g1"]])
                    rel = [e1]
                    evs["g2"] = e1
                ppj_rel[pj] = rel
                rs_readers += rel
            xT_rel[j] = [mm_last]
            stat_rel[j] = rs_readers
            o1 = s.dma("sp", lambda g, b=b, j=j: g.dma_start(out=vs_d[b * 128:(b + 1) * 128, :], in_=vsb[j][:]),
                       ("o1", j), evs["vs"])
            o2 = s.dma("sp", lambda g, b=b, j=j: g.dma_start(out=gs_d[b * 128:(b + 1) * 128, :], in_=gsb[j][:]),
                       ("o2", j), [evs["g0"], evs["g1"], evs["g2"]])
            o3 = s.dma("sp", lambda g, b=b, j=j: g.dma_start(
                out=vd_d[b * 128:(b + 1) * 128, :], in_=vdb[j][:].rearrange("p h d -> p (h d)")), ("o3", j), evs["vd"])
            vsb_rel[j] = [o1]
            gsb_rel[j] = [o2]
            vdb_rel[j] = [o3]
            outs += [o1, o2, o3]
            pq = b % 2
            t_sb = []
            for i in range(4):
                t_sb.append(s.op("pe", lambda g, i=i, pq=pq: g.transpose(
                    out=pqT[pq][:, i * 128:(i + 1) * 128], in_=qks[:, i * 128:(i + 1) * 128], identity=ident[:]),
                    evs["qks"] + pqT_rel[pq]))
            prev["qks"] = [t_sb[-1]]
            d_sq = s.op("dve", lambda g: g.tensor_tensor(out=sq[:], in0=qkd[:], in1=qkd[:], op=ALU.mult),
                        evs["qkd"] + prev.get("sq", []))
            d_ss = s.op("dve", lambda g: g.tensor_reduce(out=ssh[:], in_=sq[:].rearrange("p (h d) -> p h d", d=64),
                                                         axis=AX.X, op=ALU.add), [d_sq] + prev.get("ssh", []))
            a_l2 = s.op("act", lambda g: g.activation(out=rsh[:], in_=ssh[:], func=AF.Ln, scale=1.0 / 64, bias=EPS),
                        [d_ss] + prev.get("rsh", []))
            a_r2 = s.op("act", lambda g: g.activation(out=rsh[:], in_=rsh[:], func=AF.Exp, scale=-0.5), [a_l2])
            prev["ssh"] = [a_l2]
            d_n1 = s.op("dve", lambda g: g.tensor_tensor(
                out=sq[:].rearrange("p (h d) -> p h d", d=64), in0=qkd[:].rearrange("p (h d) -> p h d", d=64),
                in1=rsh[:].unsqueeze(2).to_broadcast([128, 24, 64]), op=ALU.mult), [a_r2, d_ss])
            prev["rsh"] = [d_n1]
            prev["qkd"] = [d_n1]
            d_n2 = s.op("dve", lambda g: g.tensor_tensor(out=sq[:], in0=sq[:], in1=gqk[:], op=ALU.mult), [d_n1, c_gq])
            sq3 = sq[:].rearrange("p (h d) -> p h d", d=64)
            qkb3 = qkb[:].rearrange("p (h d) -> p h d", d=64)
            cosb = cs[:, b, 0:8].unsqueeze(1).to_broadcast([128, 24, 8])
            sinb = cs[:, b, 8:16].unsqueeze(1).to_broadcast([128, 24, 8])
            pr = prev.get("rt", [])
            r0 = s.op("pool", lambda g, cosb=cosb: g.tensor_tensor(out=rt[0][:], in0=sq3[:, :, 0:8], in1=cosb, op=ALU.mult),
                      [d_n2, l_cs] + pr)
            r1 = s.op("pool", lambda g, sinb=sinb: g.tensor_tensor(out=rt[1][:], in0=sq3[:, :, 8:16], in1=sinb, op=ALU.mult),
                      [d_n2, l_cs] + pr)
            r2 = s.op("pool", lambda g, cosb=cosb: g.tensor_tensor(out=rt[2][:], in0=sq3[:, :, 8:16], in1=cosb, op=ALU.mult),
                      [d_n2] + pr)
            r3 = s.op("pool", lambda g, sinb=sinb: g.tensor_tensor(out=rt[3][:], in0=sq3[:, :, 0:8], in1=sinb, op=ALU.mult),
                      [d_n2] + pr)
            pk = prev.get("qkb", [])
            r4 = s.op("pool", lambda g: g.tensor_tensor(out=qkb3[:, :, 0:8], in0=rt[0][:], in1=rt[1][:], op=ALU.subtract),
                      [r0, r1] + pk)
            r5 = s.op("pool", lambda g: g.tensor_tensor(out=qkb3[:, :, 8:16], in0=rt[2][:], in1=rt[3][:], op=ALU.add),
                      [r2, r3] + pk)
            prev["rt"] = [r4, r5]
            d_cp = s.op("dve", lambda g: g.tensor_copy(out=qkb3[:, :, 16:64], in_=sq3[:, :, 16:64]), [d_n2] + pk)
            prev["sq"] = [d_cp, r0, r1, r2, r3]
            t_d = []
            for i in range(4):
                t_d.append(s.op("pe", lambda g, i=i, pq=pq: g.transpose(
                    out=pqT[pq][:, 512 + i * 128:512 + (i + 1) * 128], in_=qkb[:, i * 128:(i + 1) * 128],
                    identity=ident[:]), [r4, r5, d_cp, t_sb[-1]]))
            bs = slice(b * 128, (b + 1) * 128)
            ev1 = s.op("act", lambda g, pq=pq, j=j: g.copy(
                out=qTs[j][:], in_=pqT[pq][:, 0:256].rearrange("p (a t) -> p a t", t=128)), [t_d[-1]] + tq_rel[j])
            ev2 = s.op("act", lambda g, pq=pq, j=j: g.copy(
                out=kTs[j][:], in_=pqT[pq][:, 256:512].rearrange("p (a t) -> p a t", t=128)), [t_d[-1]] + tq_rel[j])
            ev3 = s.op("dve", lambda g, pq=pq, j=j: g.tensor_copy(
                out=qTd[j][0:64, 0:8:2, :], in_=pqT[pq][0:64, 512:1024].rearrange("p (a t) -> p a t", t=128)),
                [t_d[-1], ev2] + tq_rel[j])
            ev3 = s.op("dve", lambda g, pq=pq, j=j: g.tensor_copy(
                out=qTd[j][64:128, 1:8:2, :], in_=pqT[pq][64:128, 512:1024].rearrange("p (a t) -> p a t", t=128)),
                [ev3])
            pqT_rel[pq] = [ev1, ev2, ev3]
            po = 1 - pq
            t_e = []
            for i in range(8):
                t_e.append(s.op("pe", lambda g, i=i, po=po: g.transpose(
                    out=pqT[po][:, i * 128:(i + 1) * 128], in_=qkb[:, (4 + i) * 128:(5 + i) * 128],
                    identity=ident[:]), [r4, r5, d_cp] + pqT_rel[po]))
            prev["qkb"] = [t_e[-1]]
            ev4 = s.op("act", lambda g, po=po, j=j: g.copy(
                out=qTd[j][0:64, 8:12:2, :], in_=pqT[po][0:64, 0:256].rearrange("p (a t) -> p a t", t=128)),
                [t_e[-1]] + tq_rel[j])
            ev4 = s.op("act", lambda g, po=po, j=j: g.copy(
                out=qTd[j][64:128, 9:12:2, :], in_=pqT[po][64:128, 0:256].rearrange("p (a t) -> p a t", t=128)),
                [ev4])
            ev5 = s.op("dve", lambda g, po=po, j=j: g.tensor_copy(
                out=kTd[j][:], in_=pqT[po][:, 256:1024].rearrange("p (a t) -> p a t", t=128)), [t_e[-1], ev4] + tq_rel[j])
            pqT_rel[po] = [ev4, ev5]
            f1 = s.dma("sp", lambda g, j=j, bs=bs: g.dma_start(out=qTs_d[:, :, bs], in_=qTs[j][:]), ("f1", j), [ev1])
            f2 = s.dma("sp", lambda g, j=j, bs=bs: g.dma_start(out=kTs_d[:, :, bs], in_=kTs[j][:]), ("f2", j), [ev2])
            f3 = s.dma("sp", lambda g, j=j, bs=bs: g.dma_start(out=qTd_d[:, :, bs], in_=qTd[j][:]), ("f3", j), [ev3, ev4])
            f4 = s.dma("sp", lambda g, j=j, bs=bs: g.dma_start(out=kTd_d[:, :, bs], in_=kTd[j][:]), ("f4", j), [ev5])
            tq_rel[j] = [f1, f2, f3, f4]
            outs += [f1, f2, f3, f4]
        s.emit(final_waits=outs[-14:])


def rope_tables():
    pos = np.arange(SEQ, dtype=np.float32)
    inv = (1.0 / (np.float32(500000.0) ** (np.arange(8, dtype=np.float32) * np.float32(2.0) / np.float32(16)))).astype(np.float32)
    ang = (pos[:, None] * inv[None, :]).astype(np.float32).astype(np.float64)
    return np.cos(ang).astype(np.float32), np.sin(ang).astype(np.float32)


_ROPE = None


def host_P_inputs(norm_g_l, gq_l, gk_l, c):
    global _ROPE
    if _ROPE is None:
        _ROPE = rope_tables()
    co, si = _ROPE
    cs = np.concatenate([co[c * CH:(c + 1) * CH], si[c * CH:(c + 1) * CH]], -1)
    cs = np.ascontiguousarray(cs.reshape(NB, 128, 16).transpose(1, 0, 2))
    wg = np.ascontiguousarray(norm_g_l.reshape(8, 128).T)
    gqk = np.concatenate([np.tile(gq_l, 12), np.tile(gk_l, 12)])[None, :].repeat(128, 0)
    ident = np.eye(128, dtype=np.float32).astype(ml_dtypes.bfloat16)
    return dict(wg=wg.astype(np.float32), gqk=np.ascontiguousarray(gqk.astype(np.float32)), cs=cs.astype(np.float32),
                ident=ident)


def emit_S(nc, qz_d, kT_d, v_d, masks_d, ntri_d, ident_d, oa_d, npairs=8, qload=None, kT_src=None, v_src=None):
    with contextlib.ExitStack() as st:
        u_ = "S" + _uid() + "_"
        eset_ = u_
        T = lambda n, s, d: st.enter_context(nc.sbuf_tensor(u_ + n, s, d))
        PS = lambda n, s, d: st.enter_context(nc.psum_tensor(u_ + n, s, d))
        nkb = 16 * npairs
        kT = T("kT", [128, 2, nkb * 128], BF16)
        V = T("V", [128, nkb, 256], BF16)
        qz = T("qz", [128, 2, 2 * npairs * 128], BF16)
        qzp = T("qzp", [128, 2 * npairs, 4, 128], BF16)
        masks = T("masks", [128, 33, 128], BF16)
        ntri = T("ntri", [128, 128], BF16)
        ident = T("ident", [128, 128], BF16)
        ones = T("ones", [128, 2], BF16)
        ebuf = [T(f"e{i}", [128, 1024], F32) for i in range(2)]
        spb = [T(f"sp{i}", [128, 1024], BF16) for i in range(2)]
        ab = [T(f"a{i}", [128, 1024], BF16) for i in range(2)]
        tmp = [T(f"tmp{i}", [128, 512], F32) for i in range(2)]
        Oacc = T("Oacc", [128, 2 * npairs, 256], F32)
        Rpos = T("Rpos", [128, 2 * npairs, 4], F32)
        expR = [T(f"expR{i}", [128, 2, 4], F32) for i in range(2)]
        zps = [PS(f"z{i}", [128, 1024], F32) for i in range(2)]
        ops_ = [[PS(f"o{i}{a}", [128, 512], F32) for a in range(2)] for i in range(2)]

        s = Sched(nc, u_, eset_)
        l_c = s.dma("sp", lambda g: g.dma_start(out=masks[:], in_=masks_d), "c")
        l_c = s.dma("sp", lambda g: g.dma_start(out=ntri[:], in_=ntri_d), "c")
        l_c = s.dma("sp", lambda g: g.dma_start(out=ident[:], in_=ident_d), "c")
        if qload is None:
            l_c = s.dma("sp", lambda g: g.dma_start(out=qz[:], in_=qz_d[:, :, 0:2 * npairs * 128]), "c")
        else:
            idx = T("idx", [128, NIDX], I32)
            l_c = s.dma("sp", lambda g: g.dma_start(out=idx[:], in_=qload["idx"]), "c")
            l_q = None
            for slot in range(2 * npairs):
                for a in range(2):
                    l_q = s.dma("pool", lambda g, slot=slot, a=a: g.indirect_dma_start(
                        out=qz[:, a, slot * 128:(slot + 1) * 128], out_offset=None, in_=qload["rows128"],
                        in_offset=bass.IndirectOffsetOnAxis(ap=idx[:, slot * 2 + a:slot * 2 + a + 1], axis=0)),
                        "cq", [l_c])
            join = T("join", [128, 2], BF16)
            l_c = s.op("pool", lambda g: g.memset(join[:], 0.0), [l_c, l_q])
        i_0 = s.op("dve", lambda g: g.memset(qzp[:], 0.0))
        for h in range(4):
            pr = slice((h % 2) * 64, (h % 2) * 64 + 64)
            l_c = s.op("dve", lambda g, h=h, pr=pr: g.tensor_copy(
                out=qzp[pr, :, h, :], in_=qz[pr, h // 2, :].rearrange("p (s t) -> p s t", t=128)), [l_c, i_0])
        i_1 = s.op("pool", lambda g: g.memset(ones[:], 1.0))
        i_2 = s.op("pool", lambda g: g.memset(Oacc[:], 0.0))
        i_3 = s.op("pool", lambda g: g.memset(Rpos[:], 0.0))
        l_kv = []
        for m in range(npairs):
            cs_ = slice(m * 2048, (m + 1) * 2048)
            ksrc = kT_d[:, :, cs_] if kT_src is None else kT_src(m)
            vsrc = v_d[cs_, :] if v_src is None else v_src(m)
            l1 = s.dma("sp", lambda g, cs_=cs_, ksrc=ksrc: g.dma_start(out=kT[:, :, cs_], in_=ksrc), ("kv", m))
            l2 = s.dma("sp", lambda g, m=m, vsrc=vsrc: g.dma_start(
                out=V[:, m * 16:(m + 1) * 16, :], in_=vsrc.rearrange("(n p) c -> p n c", p=128)), ("kv", m))
            l_kv.append(l2)

        steps = [(m, n) for m in range(npairs) for n in range(16 * m + 15, -1, -1)]
        NS = len(steps)
        A1 = [None] * NS; A2 = [None] * NS; A3 = [None] * NS; X = [None] * NS
        QK = [None] * NS; CS = [None] * NS; TT = [None] * NS; AV = [None] * NS
        ACC = [None] * NS; RU = [None] * NS; PA = [None] * NS

        def emit_qk(k):
            m, n = steps[k]
            zb = zps[k % 2]
            deps = [l_c, l_kv[m]] + ([A3[k - 2]] if k >= 2 else [])
            last = None
            for a in range(2):
                slot = 2 * m + a
                for g2 in range(2):
                    last = s.op("pe", lambda g, zb=zb, a=a, g2=g2, n=n, slot=slot: g.matmul(
                        zb[:, a * 512 + g2 * 256:a * 512 + (g2 + 1) * 256], lhsT=kT[:, g2, n * 128:(n + 1) * 128],
                        rhs=qzp[:, slot, 2 * g2:2 * g2 + 2, :],
                        start=(g2 == 0), stop=False, skip_group_check=True), deps)
                    deps = []
                if n >= 16 * m:
                    mi = 2 * (n - 16 * m) + a
                    for h in range(4):
                        col = slice(a * 512 + h * 128, a * 512 + (h + 1) * 128)
                        last = s.op("pe", lambda g, zb=zb, col=col, mi=mi: g.matmul(
                            zb[:, col], lhsT=ident[:], rhs=masks[:, mi, :],
                            start=False, stop=False, skip_group_check=True))
            QK[k] = last

        def emit_pe_mid(k):
            m, n = steps[k]
            zb = zps[k % 2]
            for a in range(2):
                CS[k] = s.op("pe", lambda g, zb=zb, a=a, k=k: g.matmul(
                    zb[:, a * 512:(a + 1) * 512], lhsT=ntri[:], rhs=spb[k % 2][:, a * 512:(a + 1) * 512],
                    start=False, stop=True, skip_group_check=True), [A2[k]])
            deps = [i_1] + ([ACC[k - 2], RU[k - 2]] if k >= 2 else [])
            for a in range(2):
                ob = ops_[k % 2][a]
                for h in range(4):
                    TT[k] = s.op("pe", lambda g, ob=ob, a=a, h=h, k=k: g.matmul(
                        ob[:, 256 + 2 * h:258 + 2 * h], lhsT=spb[k % 2][:, a * 512 + h * 128:a * 512 + (h + 1) * 128],
                        rhs=ones[:], start=(h == 0), stop=False, skip_group_check=True), deps)
                    deps = []

        def emit_av(k):
            m, n = steps[k]
            for a in range(2):
                ob = ops_[k % 2][a]
                for h in range(4):
                    AV[k] = s.op("pe", lambda g, ob=ob, a=a, h=h, k=k, n=n: g.matmul(
                        ob[:, h * 64:(h + 1) * 64], lhsT=ab[k % 2][:, a * 512 + h * 128:a * 512 + (h + 1) * 128],
                        rhs=V[:, n, h * 64:(h + 1) * 64], start=False, stop=(h == 3), skip_group_check=True),
                        [A3[k], TT[k], RU[k]])

        def emit_a1(k):
            A1[k] = s.op("act", lambda g, k=k: g.activation(out=ebuf[k % 2][:], in_=zps[k % 2][:], func=AF.Exp),
                         [QK[k]] + ([A2[k - 2]] if k >= 2 else []))

        def emit_a2(k):
            A2[k] = s.op("act", lambda g, k=k: g.activation(out=spb[k % 2][:], in_=ebuf[k % 2][:], func=AF.Ln, bias=1.0),
                         [A1[k]] + ([CS[k - 2], TT[k - 2]] if k >= 2 else []))

        def emit_x(k):
            m, n = steps[k]
            deps = [i_3] + ([RU[k - 1]] if k >= 1 else []) + ([ACC[k - 2]] if k >= 2 else [])
            X[k] = s.op("act", lambda g, k=k, m=m: g.activation(out=expR[k % 2][:], in_=Rpos[:, 2 * m:2 * m + 2, :],
                                                                 func=AF.Exp, scale=-1.0), deps)

        def emit_a3(k):
            A3[k] = s.op("act", lambda g, k=k: g.activation(out=ab[k % 2][:], in_=zps[k % 2][:], func=AF.Exp),
                         [CS[k]] + ([AV[k - 2]] if k >= 2 else []))

        def emit_ru(k):
            m, n = steps[k]
            for a in range(2):
                ob = ops_[k % 2][a]
                RU[k] = s.op("dve", lambda g, ob=ob, a=a, m=m: g.tensor_tensor(
                    out=Rpos[:, 2 * m + a, :], in0=Rpos[:, 2 * m + a, :],
                    in1=ob[:, 256:264].rearrange("p (h t) -> p h t", t=2)[:, :, 0], op=ALU.add),
                    [TT[k], X[k], i_3])

        def emit_acc(k):
            m, n = steps[k]
            d0 = None
            for a in range(2):
                ob = ops_[k % 2][a]
                d0 = s.op("dve", lambda g, ob=ob, a=a, k=k: g.tensor_tensor(
                    out=tmp[k % 2][:, a * 256:(a + 1) * 256].rearrange("p (h d) -> p h d", d=64),
                    in0=ob[:, 0:256].rearrange("p (h d) -> p h d", d=64),
                    in1=expR[k % 2][:, a, :].unsqueeze(2).to_broadcast([128, 4, 64]), op=ALU.mult),
                    [AV[k], X[k]] + ([PA[k - 2]] if k >= 2 else []))
            ACC[k] = d0
            PA[k] = s.op("pool", lambda g, k=k, m=m: g.tensor_tensor(
                out=Oacc[:, 2 * m:2 * m + 2, :].rearrange("p a c -> p (a c)"),
                in0=Oacc[:, 2 * m:2 * m + 2, :].rearrange("p a c -> p (a c)"), in1=tmp[k % 2][:], op=ALU.add),
                [ACC[k], i_2] + ([PA[k - 1]] if k >= 1 else []))

        outs = []
        emit_qk(0)
        emit_a1(0)
        emit_a2(0)
        if NS > 1:
            emit_qk(1)
        for k in range(NS):
            if k >= 1:
                emit_av(k - 1)
            emit_pe_mid(k)
            if k >= 1:
                emit_acc(k - 1)
            if k + 1 < NS:
                emit_a1(k + 1)
            emit_x(k)
            emit_ru(k)
            emit_a3(k)
            if k + 1 < NS:
                emit_a2(k + 1)
            if k + 2 < NS:
                emit_qk(k + 2)
            if k >= 1:
                m1, n1 = steps[k - 1]
                if n1 == 0:
                    outs.append(s.dma("sp", lambda g, m1=m1: g.dma_start(
                        out=oa_d[m1 * 256:(m1 + 1) * 256, :].rearrange("(a p) c -> p a c", p=128),
                        in_=Oacc[:, 2 * m1:2 * m1 + 2, :]), ("o", m1), [PA[k - 1]]))
        emit_av(NS - 1)
        emit_acc(NS - 1)
        m1 = steps[NS - 1][0]
        outs.append(s.dma("sp", lambda g, m1=m1: g.dma_start(
            out=oa_d[m1 * 256:(m1 + 1) * 256, :].rearrange("(a p) c -> p a c", p=128),
            in_=Oacc[:, 2 * m1:2 * m1 + 2, :]), ("o", m1), [PA[NS - 1]]))
        s.emit(final_waits=outs)


def host_S_consts(c):
    sidx = np.arange(128)[:, None]
    tidx = np.arange(128)[None, :]
    diag = np.where(sidx >= tidx, NEG, 0.0).astype(np.float32)
    full = np.full((128, 128), NEG, np.float32)
    zero = np.zeros((128, 128), np.float32)
    mk = np.zeros((128, 33, 128), np.float32)
    for j in range(16):
        for a, cc in enumerate((c, 15 - c)):
            mk[:, 2 * j + a, :] = zero if j < cc else (diag if j == cc else full)
    ntri = -(sidx >= tidx).astype(np.float32)
    return dict(masks=mk.astype(ml_dtypes.bfloat16), ntri=ntri.astype(ml_dtypes.bfloat16))


def dil_masks():
    jk = np.arange(128)[:, None]
    iq = np.arange(128)[None, :]
    mk = np.zeros((128, 19, 128), np.float32)
    for dl in range(17):
        d = 128 * dl + iq - jk
        mk[:, dl, :] = ((d >= 0) & (d <= 128)).astype(np.float32) + ((d >= 0) & (d % 4 == 0) & (d <= 512)) \
            + ((d >= 0) & (d % 16 == 0) & (d <= 2048))
    return mk.astype(ml_dtypes.bfloat16)


def emit_D(nc, qT_d, kT_d, v_d, masks_d, ident_d, od_d, nblocks=NB, nheads=12, fused=None):
    with contextlib.ExitStack() as st:
        u_ = "D" + _uid() + "_"
        eset_ = u_
        T = lambda n, s, d: st.enter_context(nc.sbuf_tensor(u_ + n, s, d))
        PS = lambda n, s, d: st.enter_context(nc.psum_tensor(u_ + n, s, d))
        qT = T("qT", [128, 12, CH], BF16)
        kT = T("kT", [128, 6, 2 * CH], BF16)
        V = T("V", [128, 32, 960], BF16)
        masks = T("masks", [128, 19, 128], BF16)
        px = [T(f"px{i}", [128, 512], BF16) for i in range(3)]
        pb = [T(f"p{i}", [128, 512], BF16) for i in range(3)]
        rec = [T(f"rec{i}", [128, 12], F32) for i in range(2)]
        odb = [T(f"od{i}", [128, 768], F32) for i in range(2)]
        sps = [PS(f"s{i}", [128, 512], F32) for i in range(3)]
        ops_ = [PS(f"o{i}", [128, 2, 512], F32) for i in range(2)]

        s = Sched(nc, u_, eset_)
        l_c = s.dma("sp", lambda g: g.dma_start(out=masks[:], in_=masks_d), "c")
        l_c = s.dma("sp", lambda g: g.dma_start(out=qT[:], in_=qT_d), "c")
        if fused is None:
            l_c = s.dma("sp", lambda g: g.dma_start(out=kT[:], in_=kT_d), "c")
            for i in range(4):
                l_c = s.dma("sp", lambda g, i=i: g.dma_start(
                    out=V[:, i * 8:(i + 1) * 8, :], in_=v_d[i * 1024:(i + 1) * 1024, :].rearrange("(n p) c -> p n c", p=128)), "c")
        else:
            hsc = T("hsc", [128, 1], F32)
            l_c = s.dma("sp", lambda g: g.dma_start(out=hsc[:], in_=fused["hscale"]), "c")
            l_c = s.dma("sp", lambda g: g.dma_start(out=kT[:, :, CH:2 * CH], in_=fused["kown"]), "c")
            idx = T("idx", [128, NIDX], I32)
            l_c = s.dma("sp", lambda g: g.dma_start(out=idx[:], in_=fused["idx"]), "c")
            for i in range(2):
                l_c = s.dma("sp", lambda g, i=i: g.dma_start(
                    out=V[:, 16 + i * 8:16 + (i + 1) * 8, :],
                    in_=fused["vown"][i * 1024:(i + 1) * 1024, :].rearrange("(n p) c -> p n c", p=128)), "c")
            l_h = None
            for a in range(6):
                l_h = s.dma("pool", lambda g, a=a: g.indirect_dma_start(
                    out=kT[:, a, 0:CH], out_offset=None, in_=fused["rows2048"],
                    in_offset=bass.IndirectOffsetOnAxis(ap=idx[:, 32 + a:33 + a], axis=0)), "ch", [l_c])
            for n in range(16):
                l_h = s.dma("pool", lambda g, n=n: g.indirect_dma_start(
                    out=V[:, n, :], out_offset=None, in_=fused["vrows"],
                    in_offset=bass.IndirectOffsetOnAxis(ap=idx[:, 38 + n:39 + n], axis=0)), "ch", [l_c])
            l_c = s.op("pool", lambda g: g.tensor_scalar(
                out=V[:, 0:16, :].rearrange("p n c -> p (n c)"), in0=V[:, 0:16, :].rearrange("p n c -> p (n c)"),
                scalar1=hsc[:, 0:1], scalar2=None, op0=ALU.mult), [l_c, l_h])

        s_rel = [[], [], []]
        px_rel = [[], [], []]
        p_rel = [[], [], []]
        o_rel = [[], []]
        od_rel = [[], []]
        outs = []
        groups = [(d, min(d + 2, 17)) for d in range(0, 17, 2)]
        G = [(b, g6, d0_, d1_) for b in range(nblocks) for g6 in range(nheads // 2) for (d0_, d1_) in groups]
        MK = [None] * len(G)
        first_in_bank = {}
        last_av = {}
        LAG = 2

        def front(i):
            b, g6, d0_, d1_ = G[i]
            nt = d1_ - d0_
            r = i % 3
            deps = [l_c] + s_rel[r]
            mm = None
            for ti in range(nt):
                kb = 16 + b - (d0_ + ti)
                mm = s.op("pe", lambda g, r=r, ti=ti, g6=g6, kb=kb, b=b: g.matmul(
                    sps[r][:, ti * 256:(ti + 1) * 256], lhsT=kT[:, g6, kb * 128:(kb + 1) * 128],
                    rhs=qT[:, 2 * g6:2 * g6 + 2, b * 128:(b + 1) * 128],
                    start=(ti == 0), stop=True, skip_group_check=True), deps)
                deps = []
            w = nt * 256
            ex = s.op("act", lambda g, r=r, w=w: g.activation(out=px[r][:, 0:w], in_=sps[r][:, 0:w], func=AF.Exp),
                      [mm] + px_rel[r])
            s_rel[r] = [ex]
            mk = s.op("dve", lambda g, r=r, nt=nt, w=w, d0_=d0_: g.tensor_tensor(
                out=pb[r][:, 0:w].rearrange("p (t j q) -> p t j q", j=2, q=128),
                in0=px[r][:, 0:w].rearrange("p (t j q) -> p t j q", j=2, q=128),
                in1=masks[:, d0_:d0_ + nt, :].unsqueeze(2).to_broadcast([128, nt, 2, 128]), op=ALU.mult),
                [ex] + p_rel[r])
            px_rel[r] = [mk]
            MK[i] = mk

        def back(i):
            b, g6, d0_, d1_ = G[i]
            nt = d1_ - d0_
            r = i % 3
            ob = ops_[b % 2]
            av = None
            for ti in range(nt):
                kb = 16 + b - (d0_ + ti)
                dl = d0_ + ti
                for j2 in range(2):
                    h = 2 * g6 + j2
                    bank = h // 6
                    ocol = (h % 6) * 80
                    fb = first_in_bank.get((b, bank), True)
                    deps = [MK[i]] + (o_rel[b % 2] if fb else [])
                    av = s.op("pe", lambda g, ob=ob, bank=bank, ocol=ocol, r=r, ti=ti, j2=j2, kb=kb, h=h, fb=fb, dl=dl: g.matmul(
                        ob[:, bank, ocol:ocol + 80], lhsT=pb[r][:, ti * 256 + j2 * 128:ti * 256 + (j2 + 1) * 128],
                        rhs=V[:, kb, h * 80:(h + 1) * 80], start=fb, stop=(dl == 16), skip_group_check=True), deps)
                    first_in_bank[(b, bank)] = False
            p_rel[r] = [av]
            last_av[b] = av
            if i + 1 == len(G) or G[i + 1][0] != b:
                j = b % 2
                o4 = ob[:, :, 0:480].rearrange("p a (h d) -> p a h d", d=80)
                d1 = s.op("dve", lambda g, o4=o4, j=j: g.reciprocal(
                    out=rec[j][:].rearrange("p (a h) -> p a h", a=2), in_=o4[:, :, :, 64]), [av] + od_rel[j])
                d2 = None
                for a in range(2):
                    d2 = s.op("dve", lambda g, o4=o4, j=j, a=a: g.tensor_tensor(
                        out=odb[j][:, a * 384:(a + 1) * 384].rearrange("p (h d) -> p h d", d=64),
                        in0=o4[:, a, :, 0:64], in1=rec[j][:, a * 6:(a + 1) * 6].unsqueeze(2).to_broadcast([128, 6, 64]),
                        op=ALU.mult), [d1])
                o_rel[b % 2] = [d2]
                od_ = s.dma("sp", lambda g, b=b, j=j: g.dma_start(out=od_d[b * 128:(b + 1) * 128, :], in_=odb[j][:]),
                            ("o", j), [d2])
                od_rel[j] = [od_]
                outs.append(od_)

        for i in range(len(G) + LAG):
            if i < len(G):
                front(i)
            if i - LAG >= 0:
                back(i - LAG)
        s.emit(final_waits=outs[-2:])


def emit_O(nc, x_d, oa_d, od_d, gs_d, wo_d, ident_d, xo_d, nblocks=NB, oa_load=None):
    with contextlib.ExitStack() as st:
        u_ = "O" + _uid() + "_"
        eset_ = "misc"
        T = lambda n, s, d: st.enter_context(nc.sbuf_tensor(u_ + n, s, d))
        PS = lambda n, s, d: st.enter_context(nc.psum_tensor(u_ + n, s, d))
        Wo = T("Wo", [128, 8, 1024], BF16)
        wst = [T(f"wst{i}", [128, 1024], F32) for i in range(2)]
        ident = T("ident", [128, 128], BF16)
        xs = [T(f"xs{i}", [128, 1024], F32) for i in range(2)]
        att = [T(f"att{i}", [128, 1024], F32) for i in range(2)]
        gsb = [T(f"gs{i}", [128, 1024], F32) for i in range(2)]
        mix = T("mix", [128, 1024], BF16)
        mixT = T("mixT", [128, 8, 128], BF16)
        xo = [T(f"xo{i}", [128, 1024], F32) for i in range(2)]
        pT = PS("pT", [128, 1024], BF16)
        py = [PS(f"py{i}", [128, 512], F32) for i in range(2)]

        s = Sched(nc, u_, eset_)
        l_id = s.dma("sp", lambda g: g.dma_start(out=ident[:], in_=ident_d), "c")
        if oa_load is not None:
            idx = T("idx", [128, NIDX], I32)
            l_idx = s.dma("sp", lambda g: g.dma_start(out=idx[:], in_=oa_load["idx"]), "ci")
        wrel = [None, None]
        wc = []
        for kc in range(8):
            j = kc % 2
            ld = s.dma("pool", lambda g, kc=kc, j=j: g.dma_start(out=wst[j][:], in_=wo_d[kc * 128:(kc + 1) * 128, :]),
                       ("w", j), [wrel[j]])
            c_ = s.op("pool", lambda g, kc=kc, j=j: g.tensor_copy(out=Wo[:, kc, :], in_=wst[j][:]), [ld])
            wrel[j] = c_
            wc.append(c_)
        in_rel = [[], []]
        xo_rel = [[], []]
        prev = {}
        outs = []
        for b in range(nblocks):
            j = b % 2
            rs = slice(b * 128, (b + 1) * 128)
            l1 = s.dma("sp", lambda g, j=j, rs=rs: g.dma_start(out=xs[j][:], in_=x_d[rs, :]), ("i", j), in_rel[j])
            if oa_load is None:
                l2 = s.dma("sp", lambda g, j=j, rs=rs: g.dma_start(out=att[j][:, 0:256], in_=oa_d[rs, :]), ("i", j))
            else:
                l2 = s.dma("pool", lambda g, j=j, b=b: g.indirect_dma_start(
                    out=att[j][:, 0:256], out_offset=None, in_=oa_load["rows"],
                    in_offset=bass.IndirectOffsetOnAxis(ap=idx[:, 54 + b:55 + b], axis=0)), ("ia", j), [l_idx] + in_rel[j])
            l3 = s.dma("sp", lambda g, j=j, rs=rs: g.dma_start(out=att[j][:, 256:1024], in_=od_d[rs, :]), ("i", j))
            l4 = s.dma("sp", lambda g, j=j, rs=rs: g.dma_start(out=gsb[j][:], in_=gs_d[rs, :]), ("i", j))
            d1 = s.op("dve", lambda g, j=j: g.tensor_tensor(out=mix[:], in0=att[j][:], in1=gsb[j][:], op=ALU.mult),
                      [l4, l2] + prev.get("mix", []))
            tp = None
            for kc in range(8):
                tp = s.op("pe", lambda g, kc=kc: g.transpose(out=pT[:, kc * 128:(kc + 1) * 128],
                                                             in_=mix[:, kc * 128:(kc + 1) * 128], identity=ident[:]),
                          [d1, l_id] + prev.get("pT", []))
            prev["mix"] = [tp]
            c1 = s.op("act", lambda g: g.copy(out=mixT[:].rearrange("p a t -> p (a t)"), in_=pT[:]),
                      [tp] + prev.get("mixT", []))
            prev["pT"] = [c1]
            ev = []
            mm = None
            for hf in range(2):
                for kc in range(8):
                    mm = s.op("pe", lambda g, hf=hf, kc=kc: g.matmul(
                        py[hf][:], lhsT=mixT[:, kc, :], rhs=Wo[:, kc, hf * 512:(hf + 1) * 512],
                        start=(kc == 0), stop=(kc == 7)), ([c1, wc[kc]] + prev.get(("py", hf), [])) if kc == 0 else [wc[kc]])
                e_ = s.op("dve", lambda g, hf=hf, j=j: g.tensor_tensor(
                    out=xo[j][:, hf * 512:(hf + 1) * 512], in0=xs[j][:, hf * 512:(hf + 1) * 512], in1=py[hf][:],
                    op=ALU.add), [mm, l4] + xo_rel[j])
                prev[("py", hf)] = [e_]
                ev.append(e_)
            prev["mixT"] = [mm]
            in_rel[j] = [d1, ev[1]]
            o_ = s.dma("sp", lambda g, j=j, rs=rs: g.dma_start(out=xo_d[rs, :], in_=xo[j][:]), ("o", j), ev)
            xo_rel[j] = [o_]
            outs.append(o_)
        s.emit(final_waits=outs[-2:])


_PROG = {}
_UID = [0]


def _uid():
    _UID[0] += 1
    return str(_UID[0])
_DBG = None


def _dram(nc, n, s, d, k):
    return nc.dram_tensor(n, s, d, kind=k).ap()


def build_P():
    nc = bass.Bass("TRN2", target_bir_lowering=False)
    I, O = "ExternalInput", "ExternalOutput"
    x_d = _dram(nc, "x", [CH, 1024], F32, I)
    wg_d = _dram(nc, "wg", [128, 8], F32, I)
    win_d = _dram(nc, "win", [1024, 4096], F32, I)
    gqk_d = _dram(nc, "gqk", [128, 1536], F32, I)
    cs_d = _dram(nc, "cs", [128, NB, 16], F32, I)
    ident_d = _dram(nc, "ident", [128, 128], BF16, I)
    qTs_d = _dram(nc, "qTs", [128, 2, CH], BF16, O)
    kTs_d = _dram(nc, "kTs", [128, 2, CH], BF16, O)
    vs_d = _dram(nc, "vs", [CH, 256], BF16, O)
    qTd_d = _dram(nc, "qTd", [128, 12, CH], BF16, O)
    kTd_d = _dram(nc, "kTd", [128, 6, CH], BF16, O)
    vd_d = _dram(nc, "vd", [CH, 960], BF16, O)
    gs_d = _dram(nc, "gs", [CH, 1024], F32, O)
    emit_P(nc, None, x_d, wg_d, win_d, gqk_d, cs_d, ident_d, qTs_d, kTs_d, vs_d, qTd_d, kTd_d, vd_d, gs_d)
    return nc


def build_SD():
    nc = bass.Bass("TRN2", target_bir_lowering=False)
    I, O = "ExternalInput", "ExternalOutput"
    qz_d = _dram(nc, "qz", [128, 2, 2048], BF16, I)
    kT_d = _dram(nc, "kTs", [128, 2, SEQ], BF16, I)
    v_d = _dram(nc, "vs", [SEQ, 256], BF16, I)
    masks_d = _dram(nc, "masks", [128, 33, 128], BF16, I)
    ntri_d = _dram(nc, "ntri", [128, 128], BF16, I)
    ident_d = _dram(nc, "ident", [128, 128], BF16, I)
    oa_d = _dram(nc, "oa", [2048, 256], F32, O)
    qTd_d = _dram(nc, "qTd", [128, 12, CH], BF16, I)
    kTd_d = _dram(nc, "kTd", [128, 6, 2 * CH], BF16, I)
    vd_d = _dram(nc, "vd", [2 * CH, 960], BF16, I)
    dmask_d = _dram(nc, "dmask", [128, 19, 128], BF16, I)
    od_d = _dram(nc, "od", [CH, 768], F32, O)
    emit_S(nc, qz_d, kT_d, v_d, masks_d, ntri_d, ident_d, oa_d)
    emit_D(nc, qTd_d, kTd_d, vd_d, dmask_d, ident_d, od_d)
    return nc


def build_O():
    nc = bass.Bass("TRN2", target_bir_lowering=False)
    I, O = "ExternalInput", "ExternalOutput"
    x_d = _dram(nc, "x", [CH, 1024], F32, I)
    oa_d = _dram(nc, "oa", [CH, 256], F32, I)
    od_d = _dram(nc, "od", [CH, 768], F32, I)
    gs_d = _dram(nc, "gs", [CH, 1024], F32, I)
    wo_d = _dram(nc, "wo", [1024, 1024], F32, I)
    ident_d = _dram(nc, "ident", [128, 128], BF16, I)
    xo_d = _dram(nc, "xo", [CH, 1024], F32, O)
    emit_O(nc, x_d, oa_d, od_d, gs_d, wo_d, ident_d, xo_d)
    return nc


def _slots(c):
    out = []
    for m in range(8):
        out += [16 * m + c, 16 * m + 15 - c]
    return out


def kernel_unfused(x, norm_g, w_in, q_norm_g, k_norm_g, w_out):
    x = np.asarray(x, np.float32)
    norm_g = np.asarray(norm_g, np.float32)
    w_in = np.asarray(w_in, np.float32)
    q_norm_g = np.asarray(q_norm_g, np.float32)
    k_norm_g = np.asarray(k_norm_g, np.float32)
    w_out = np.asarray(w_out, np.float32)
    cores = list(range(NCORES))
    bf = ml_dtypes.bfloat16
    ident = np.eye(128, dtype=np.float32).astype(bf)
    dmask = dil_masks()
    sconst = [host_S_consts(c) for c in cores]
    xs = [np.ascontiguousarray(x[0, c * CH:(c + 1) * CH]) for c in cores]
    for l in range(DEPTH):
        ncP = build_P()
        maps = []
        for c in cores:
            hp = host_P_inputs(norm_g[l], q_norm_g[l], k_norm_g[l], c)
            maps.append(dict(x=xs[c], win=w_in[l], **hp))
        rP = run_bass_kernel_spmd(ncP, maps, core_ids=cores).results
        kTs_all = np.ascontiguousarray(np.concatenate([rP[c]["kTs"] for c in cores], axis=2))
        qTs_all = np.concatenate([rP[c]["qTs"] for c in cores], axis=2)
        vs_all = np.ascontiguousarray(np.concatenate([rP[c]["vs"] for c in cores], axis=0))
        ncSD = build_SD()
        maps = []
        for c in cores:
            qz = np.ascontiguousarray(np.concatenate(
                [qTs_all[:, :, b * 128:(b + 1) * 128] for b in _slots(c)], axis=2))
            if c == 0:
                kprev = np.zeros_like(rP[0]["kTd"])
                vprev = np.zeros_like(rP[0]["vd"])
            else:
                kprev, vprev = rP[c - 1]["kTd"], rP[c - 1]["vd"]
            maps.append(dict(qz=qz, kTs=kTs_all, vs=vs_all, masks=sconst[c]["masks"], ntri=sconst[c]["ntri"],
                             ident=ident, qTd=rP[c]["qTd"],
                             kTd=np.ascontiguousarray(np.concatenate([kprev, rP[c]["kTd"]], axis=2)),
                             vd=np.ascontiguousarray(np.concatenate([vprev, rP[c]["vd"]], axis=0)), dmask=dmask))
        rSD = run_bass_kernel_spmd(ncSD, maps, core_ids=cores).results
        oa_all = np.zeros((SEQ, 256), np.float32)
        for c in cores:
            for si, b in enumerate(_slots(c)):
                oa_all[b * 128:(b + 1) * 128] = rSD[c]["oa"][si * 128:(si + 1) * 128]
        ncO = build_O()
        maps = []
        for c in cores:
            maps.append(dict(x=xs[c], oa=np.ascontiguousarray(oa_all[c * CH:(c + 1) * CH]), od=rSD[c]["od"],
                             gs=rP[c]["gs"], wo=w_out[l], ident=ident))
        rO = run_bass_kernel_spmd(ncO, maps, core_ids=cores).results
        xs = [rO[c]["xo"] for c in cores]
        if _DBG is not None:
            _DBG.append(dict(x=np.concatenate(xs, 0), oa=oa_all, od=np.concatenate([rSD[c]["od"] for c in cores], 0),
                             gs=np.concatenate([rP[c]["gs"] for c in cores], 0)))
    return np.concatenate(xs, axis=0)[None].astype(np.float32)


TW = 20480


def build_fused(depth=DEPTH):
    nc = bass.Bass("TRN2", target_bir_lowering=False)
    I, O, N = "ExternalInput", "ExternalOutput", "Internal"
    x_in = _dram(nc, "x", [CH, 1024], F32, I)
    wg_d = _dram(nc, "wg", [depth, 128, 8], F32, I)
    win_d = _dram(nc, "win", [depth, 1024, 4096], F32, I)
    gqk_d = _dram(nc, "gqk", [depth, 128, 1536], F32, I)
    wo_d = _dram(nc, "wo", [depth, 1024, 1024], F32, I)
    cs_d = _dram(nc, "cs", [128, NB, 16], F32, I)
    ident_d = _dram(nc, "ident", [128, 128], BF16, I)
    masks_d = _dram(nc, "masks", [128, 33, 128], BF16, I)
    ntri_d = _dram(nc, "ntri", [128, 128], BF16, I)
    dmask_d = _dram(nc, "dmask", [128, 19, 128], BF16, I)
    hscale_d = _dram(nc, "hscale", [128, 1], F32, I)
    idx_d = _dram(nc, "idx", [128, NIDX], I32, I)
    x_out = _dram(nc, "xo", [CH, 1024], F32, O)
    rg = [list(range(NCORES))]
    sendT = [_dram(nc, f"sendT{i}", [128, TW], BF16, N) for i in range(2)]
    sendVs = [_dram(nc, f"sendVs{i}", [CH, 256], BF16, N) for i in range(2)]
    sendVd = [_dram(nc, f"sendVd{i}", [CH, 960], BF16, N) for i in range(2)]
    sendA = [_dram(nc, f"sendA{i}", [CH, 256], F32, N) for i in range(2)]
    recvT = [_dram(nc, f"recvT{i}", [NCORES * 128, TW], BF16, N) for i in range(2)]
    recvVs = [_dram(nc, f"recvVs{i}", [NCORES * CH, 256], BF16, N) for i in range(2)]
    recvVd = [_dram(nc, f"recvVd{i}", [NCORES * CH, 960], BF16, N) for i in range(2)]
    recvA = [_dram(nc, f"recvA{i}", [NCORES * CH, 256], F32, N) for i in range(2)]
    qTd_d = _dram(nc, "qTd_l", [128, 12, CH], BF16, N)
    gs_d = _dram(nc, "gs_l", [CH, 1024], F32, N)
    od_d = _dram(nc, "od_l", [CH, 768], F32, N)
    xbuf = [_dram(nc, f"xbuf{i}", [CH, 1024], F32, N) for i in range(2)]

    def allgather(name, pairs):
        sc = Sched(nc, name)
        cs_ = []
        for i, (a, b) in enumerate(pairs):
            cs_.append(sc.dma("pool", lambda g, a=a, b=b: g.collective_compute(
                "AllGather", ALU.bypass, replica_groups=rg, ins=[a.opt()], outs=[b.opt()]), f"cc{i}", inc=1))
        sc.emit(final_waits=cs_)

    for l in range(depth):
        p = l % 2
        x_src = x_in if l == 0 else xbuf[(l - 1) % 2]
        x_dst = x_out if l == depth - 1 else xbuf[l % 2]
        sT, rT = sendT[p], recvT[p]
        emit_P(nc, None, x_src, wg_d[l], win_d[l], gqk_d[l], cs_d, ident_d,
               sT[:, 0:4096].rearrange("p (a t) -> p a t", a=2), sT[:, 4096:8192].rearrange("p (a t) -> p a t", a=2),
               sendVs[p], qTd_d, sT[:, 8192:TW].rearrange("p (a t) -> p a t", a=6), sendVd[p], gs_d)
        import os
        stop = int(os.environ.get("FUSE_STOP", "9"))
        if stop < 1:
            continue
        allgather(f"X{l}", [(sT, rT), (sendVs[p], recvVs[p]), (sendVd[p], recvVd[p])])
        if stop < 2:
            continue
        if stop != 7:
          emit_S(nc, None, None, None, masks_d, ntri_d, ident_d, sendA[p],
               qload=dict(idx=idx_d, rows128=rT.rearrange("r (k c) -> (r k) c", c=128)),
               kT_src=lambda m, rT=rT: rT[m * 128:(m + 1) * 128, 4096:8192].rearrange("p (a t) -> p a t", a=2),
               v_src=lambda m, rV=recvVs[p]: rV[m * CH:(m + 1) * CH, :])
        if stop < 3:
            continue
        if stop != 7:
          emit_D(nc, qTd_d, None, None, dmask_d, ident_d, od_d,
               fused=dict(hscale=hscale_d, idx=idx_d, kown=sT[:, 8192:TW].rearrange("p (a t) -> p a t", a=6),
                          rows2048=rT.rearrange("r (k c) -> (r k) c", c=2048), vown=sendVd[p], vrows=recvVd[p]))
        if stop < 4:
            continue
        allgather(f"Y{l}", [(sendA[p], recvA[p])])
        if stop < 5:
            continue
        emit_O(nc, x_src, None, od_d, gs_d, wo_d[l], ident_d, x_dst, oa_load=dict(idx=idx_d, rows=recvA[p]))
    return nc


def host_idx(c):
    p = np.arange(128)
    idx = np.zeros((128, NIDX), np.int64)
    pr = (c + 7) % 8
    for slot in range(16):
        m, half = slot // 2, slot % 2
        cb = c if half == 0 else 15 - c
        for a in range(2):
            idx[:, slot * 2 + a] = (m * 128 + p) * 160 + a * 16 + cb
    for a in range(6):
        idx[:, 32 + a] = (pr * 128 + p) * 10 + 4 + a
    for n in range(16):
        idx[:, 38 + n] = pr * CH + n * 128 + p
    for b in range(16):
        rk = min(b, 15 - b)
        half = 0 if b <= 7 else 1
        idx[:, 54 + b] = rk * CH + (2 * c + half) * 128 + p
    return idx.astype(np.int32)


def host_fused_inputs(x, norm_g, w_in, q_norm_g, k_norm_g, w_out, depth=DEPTH):
    bf = ml_dtypes.bfloat16
    ident = np.eye(128, dtype=np.float32).astype(bf)
    dmask = dil_masks()
    maps = []
    wg = np.stack([np.ascontiguousarray(norm_g[l].reshape(8, 128).T) for l in range(depth)]).astype(np.float32)
    gqk = np.stack([np.concatenate([np.tile(q_norm_g[l], 12), np.tile(k_norm_g[l], 12)])[None, :].repeat(128, 0)
                    for l in range(depth)]).astype(np.float32)
    for c in range(NCORES):
        hp = host_P_inputs(norm_g[0], q_norm_g[0], k_norm_g[0], c)
        sc_ = host_S_consts(c)
        maps.append(dict(x=np.ascontiguousarray(x[0, c * CH:(c + 1) * CH]), wg=wg, win=w_in[:depth], gqk=gqk,
                         wo=w_out[:depth], cs=hp["cs"], ident=ident, masks=sc_["masks"], ntri=sc_["ntri"], dmask=dmask,
                         hscale=np.full((128, 1), 0.0 if c == 0 else 1.0, np.float32), idx=host_idx(c)))
    return maps


def kernel(x, norm_g, w_in, q_norm_g, k_norm_g, w_out):
    x = np.asarray(x, np.float32)
    norm_g = np.asarray(norm_g, np.float32)
    w_in = np.ascontiguousarray(np.asarray(w_in, np.float32))
    q_norm_g = np.asarray(q_norm_g, np.float32)
    k_norm_g = np.asarray(k_norm_g, np.float32)
    w_out = np.ascontiguousarray(np.asarray(w_out, np.float32))
    nc = build_fused(DEPTH)
    maps = host_fused_inputs(x, norm_g, w_in, q_norm_g, k_norm_g, w_out, DEPTH)
    res = run_bass_kernel_spmd(nc, maps, core_ids=list(range(NCORES))).results
    return np.concatenate([res[c]["xo"] for c in range(NCORES)], axis=0)[None].astype(np.float32)
```
